# Optimizing a Trainium2 kernel written in Bass

```python
import math
import jax, jax.numpy as jnp
from jax import lax
import numpy as np

D_MODEL = 1024
BATCH = 2
SEQ = 8192
DEPTH = 2
DEC_BATCH = 32
DEC_SEQ = 1
PAST_LEN = 16384
PAGE_SIZE = 128

MIX_WIDTH = D_MODEL
S5_WIDTH = MIX_WIDTH // 2
S5_GROUP = 16
S5_GROUPS = S5_WIDTH // S5_GROUP
S5_STATE = 64
DIFF_HEAD_DIM = 64
DIFF_HEADS = (MIX_WIDTH - S5_WIDTH) // (2 * DIFF_HEAD_DIM)
DIFF_QK = 2 * DIFF_HEADS * DIFF_HEAD_DIM
DIFF_V = DIFF_HEADS * 2 * DIFF_HEAD_DIM
AB_IN = S5_WIDTH + 2 * DIFF_QK + DIFF_V
LRU_WIDTH = MIX_WIDTH // 2
LRU_BLOCKS = 8
LRU_BLOCK = LRU_WIDTH // LRU_BLOCKS
LRU_C = 8.0
CONV_WIDTH = 4
POOL_WIDTH = MIX_WIDTH - LRU_WIDTH
POOL_WINDOWS = (2, 4, 8, 16)
POOL_GROUP = POOL_WIDTH // len(POOL_WINDOWS)
POOL_BUF = max(POOL_WINDOWS) - 1
CD_IN = 2 * LRU_WIDTH + POOL_WIDTH
FFN_HIDDEN = -((-8 * D_MODEL) // (3 * 256)) * 256
ROPE_THETA = 10000.0
NORM_EPS = 1e-6
SUBLN_EPS = 1e-5
Q_BLOCK = 128
N_AB_LAYERS = (DEPTH + 1) // 2
N_CD_LAYERS = DEPTH // 2

kernel_name = 'hybrid_s5_diffattn_rglru_pool_step'


def rmsnorm(x, g, eps=NORM_EPS):
    xf = x.astype(jnp.float32)
    y = xf * lax.rsqrt(jnp.mean(xf * xf, axis=-1, keepdims=True) + eps)
    return (y * g.astype(jnp.float32)).astype(x.dtype)


def rope(x, pos):
    half = x.shape[-1] // 2
    inv = ROPE_THETA ** (-jnp.arange(half, dtype=jnp.float32) / half)
    ang = pos.astype(jnp.float32)[:, None] * inv[None, :]
    cos = jnp.cos(ang)[None, :, None, :]
    sin = jnp.sin(ang)[None, :, None, :]
    xf = x.astype(jnp.float32)
    x1, x2 = xf[..., :half], xf[..., half:]
    return jnp.concatenate([x1 * cos - x2 * sin, x2 * cos + x1 * sin], axis=-1).astype(x.dtype)


def linear_scan(a, b, h0):
    def combine(left, right):
        a_l, b_l = left
        a_r, b_r = right
        return a_l * a_r, a_r * b_l + b_r
    a_cum, h = lax.associative_scan(combine, (a, b), axis=1)
    return h + a_cum * h0[:, None]


def s5_ssm(u, h0_re, h0_im, a_re, a_im, log_dt, b_re, b_im, c_re, c_im, d_skip, w_glu, b_glu):
    bsz, L, _ = u.shape
    f32 = jnp.float32
    uf = u.astype(f32)
    lam = lax.complex(a_re.astype(f32), a_im.astype(f32))
    step = jnp.exp(log_dt.astype(f32))[:, None]
    lam_bar = jnp.exp(lam * step)
    b_bar = ((lam_bar - 1.0) / lam)[:, :, None] * lax.complex(b_re.astype(f32), b_im.astype(f32))
    ug = uf.reshape(bsz, L, S5_GROUPS, S5_GROUP).astype(jnp.complex64)
    bu = jnp.einsum('blgc,gpc->blgp', ug, b_bar)
    h0 = lax.complex(h0_re.astype(f32), h0_im.astype(f32))
    h = linear_scan(jnp.broadcast_to(lam_bar, bu.shape), bu, h0)
    c = lax.complex(c_re.astype(f32), c_im.astype(f32))
    y = jnp.real(jnp.einsum('blgp,gcp->blgc', h, c)).reshape(bsz, L, S5_WIDTH) + d_skip.astype(f32) * uf
    z = jax.nn.gelu(y)
    out = z * jax.nn.sigmoid(z @ w_glu.astype(f32) + b_glu.astype(f32))
    h_last = h[:, -1]
    return out.astype(u.dtype), jnp.real(h_last), jnp.imag(h_last)


def diff_attention(q, q_pos, segments, lam, gain, lam_init):
    f32 = jnp.float32
    bsz, lq = q.shape[:2]
    qb = Q_BLOCK if lq % Q_BLOCK == 0 else lq
    nb = lq // qb
    scale = DIFF_HEAD_DIM ** -0.5
    segs = [(k.astype(f32), v.astype(f32), kp) for k, v, kp in segments]
    bounds = np.cumsum([k.shape[1] for k, _, _ in segs])[:-1].tolist()
    q_blocks = (q.astype(f32) * scale).reshape(bsz, nb, qb, 2 * DIFF_HEADS, DIFF_HEAD_DIM).swapaxes(0, 1)
    pos_blocks = q_pos.reshape(nb, qb)

    def block(args):
        qblk, pblk = args
        scores = []
        for k, _, kp in segs:
            s = jnp.einsum('bqhd,bkhd->bhqk', qblk, k)
            scores.append(jnp.where((kp[None, :] <= pblk[:, None])[None, None], s, -jnp.inf))
        p = jax.nn.softmax(jnp.concatenate(scores, axis=-1), axis=-1)
        p = p.reshape(bsz, DIFF_HEADS, 2, qb, -1)
        w = p[:, :, 0] - lam * p[:, :, 1]
        parts = jnp.split(w, bounds, axis=-1)
        o = jnp.einsum('bhqk,bkhe->bqhe', parts[0], segs[0][1])
        for wp, (_, v, _) in zip(parts[1:], segs[1:]):
            o = o + jnp.einsum('bhqk,bkhe->bqhe', wp, v)
        return o

    o = lax.map(block, (q_blocks, pos_blocks))
    o = o.swapaxes(0, 1).reshape(bsz, lq, DIFF_HEADS, 2 * DIFF_HEAD_DIM)
    o = rmsnorm(o, gain, SUBLN_EPS) * (1.0 - lam_init)
    return o.reshape(bsz, lq, DIFF_V).astype(q.dtype)


def ab_mixer(h, pos, past, h0_re, h0_im, w_in, w_out, a_re, a_im, log_dt, b_re, b_im, c_re, c_im,
             d_skip, w_glu, b_glu, lq1, lk1, lq2, lk2, subln, lam_init):
    bsz, L, _ = h.shape
    f32 = jnp.float32
    proj = h @ w_in
    u, q, k, v = jnp.split(proj, [S5_WIDTH, S5_WIDTH + DIFF_QK, S5_WIDTH + 2 * DIFF_QK], axis=-1)
    q = rope(q.reshape(bsz, L, 2 * DIFF_HEADS, DIFF_HEAD_DIM), pos)
    k = rope(k.reshape(bsz, L, 2 * DIFF_HEADS, DIFF_HEAD_DIM), pos)
    v = v.reshape(bsz, L, DIFF_HEADS, 2 * DIFF_HEAD_DIM)
    lam = (jnp.exp(jnp.sum(lq1.astype(f32) * lk1.astype(f32)))
           - jnp.exp(jnp.sum(lq2.astype(f32) * lk2.astype(f32))) + lam_init)
    segments = ([] if past is None else [past]) + [(k, v, pos)]
    attn = diff_attention(q, pos, segments, lam, subln, lam_init)
    s5_out, h_re, h_im = s5_ssm(u, h0_re, h0_im, a_re, a_im, log_dt, b_re, b_im, c_re, c_im, d_skip, w_glu, b_glu)
    out = jnp.concatenate([s5_out.astype(h.dtype), attn.astype(h.dtype)], axis=-1) @ w_out
    return out, k, v, h_re, h_im


def block_diag(x, w, b):
    bsz, L, _ = x.shape
    xb = x.reshape(bsz, L, LRU_BLOCKS, LRU_BLOCK)
    return jnp.einsum('blnc,ncd->blnd', xb, w.astype(jnp.float32)).reshape(bsz, L, LRU_WIDTH) + b.astype(jnp.float32)


def pool_mix(xp, pos, buf, pool_w, pool_scale):
    f32 = jnp.float32
    bsz, L, _ = xp.shape
    ext = jnp.concatenate([buf.astype(xp.dtype), xp], axis=1)
    ef = ext.astype(f32)
    cs = jnp.concatenate([jnp.zeros((bsz, 1, POOL_WIDTH), f32), jnp.cumsum(ef, axis=1)], axis=1)
    end = cs[:, POOL_BUF + 1:]
    xf = ef[:, POOL_BUF:]
    outs = []
    for g, w in enumerate(POOL_WINDOWS):
        sl = slice(g * POOL_GROUP, (g + 1) * POOL_GROUP)
        win = end[..., sl] - cs[:, POOL_BUF + 1 - w:POOL_BUF + 1 - w + L, sl]
        cnt = jnp.minimum(pos + 1, w).astype(f32)[None, :, None]
        outs.append(win / cnt - xf[..., sl])
    pooled = jnp.stack(outs, axis=2)
    y = jnp.einsum('blgc,gcd->blgd', pooled, pool_w.astype(f32)).reshape(bsz, L, POOL_WIDTH)
    return y * pool_scale.astype(f32), ext[:, L:]


def cd_mixer(h, pos, conv_buf, lru_h0, pool_buf, w_in, w_out, conv_w, conv_b, wa, ba, wx, bx,
             lru_lambda, pool_w, pool_scale):
    f32 = jnp.float32
    bsz, L, _ = h.shape
    proj = h @ w_in
    gate, xl, xp = jnp.split(proj, [LRU_WIDTH, 2 * LRU_WIDTH], axis=-1)
    ext = jnp.concatenate([conv_buf.astype(xl.dtype), xl], axis=1)
    xc = conv_b.astype(f32) + ext[:, 0:L].astype(f32) * conv_w[0].astype(f32)
    for j in range(1, CONV_WIDTH):
        xc = xc + ext[:, j:j + L].astype(f32) * conv_w[j].astype(f32)
    r = jax.nn.sigmoid(block_diag(xc, wa, ba))
    i = jax.nn.sigmoid(block_diag(xc, wx, bx))
    log_a = -LRU_C * r * jax.nn.softplus(-lru_lambda.astype(f32))
    a = jnp.exp(log_a)
    b = jnp.sqrt(-jnp.expm1(2.0 * log_a)) * (i * xc)
    hs = linear_scan(a, b, lru_h0.astype(f32))
    lru_out = jax.nn.gelu(gate.astype(f32)) * hs
    pool_out, new_pool = pool_mix(xp, pos, pool_buf, pool_w, pool_scale)
    out = jnp.concatenate([lru_out.astype(h.dtype), pool_out.astype(h.dtype)], axis=-1) @ w_out
    return out, ext[:, L:], hs[:, -1], new_pool


def swiglu(x, wg, wu, wd):
    return (jax.nn.silu(x @ wg) * (x @ wu)) @ wd


def setup_inputs(seed: int = 0) -> dict:
    key = jax.random.key(seed)
    keys = jax.random.split(key, 64)
    counter = [0]

    def nk():
        counter[0] += 1
        return keys[counter[0] - 1]

    def nrm(shape, scale=1.0):
        return jax.random.normal(nk(), shape, jnp.float32) * scale

    f32 = jnp.float32
    n_pages = PAST_LEN // PAGE_SIZE
    n_pool = (5 * DEC_BATCH * n_pages + 3) // 4
    nab, ncd = N_AB_LAYERS, N_CD_LAYERS
    h2, dh = 2 * DIFF_HEADS, DIFF_HEAD_DIM

    x_prompt = nrm((BATCH, SEQ, D_MODEL))
    x_sample = nrm((DEC_BATCH, DEC_SEQ, D_MODEL))
    cache_k = nrm((nab, n_pool, PAGE_SIZE, h2, dh))
    cache_v = nrm((nab, n_pool, PAGE_SIZE, DIFF_HEADS, 2 * dh))
    page_table = jax.random.permutation(nk(), n_pool)[:DEC_BATCH * n_pages].reshape(DEC_BATCH, n_pages).astype(jnp.int32)
    state_s5_re = nrm((nab, DEC_BATCH, S5_GROUPS, S5_STATE), 0.3)
    state_s5_im = nrm((nab, DEC_BATCH, S5_GROUPS, S5_STATE), 0.3)
    state_conv = nrm((ncd, DEC_BATCH, CONV_WIDTH - 1, LRU_WIDTH))
    state_lru = nrm((ncd, DEC_BATCH, LRU_WIDTH), 0.5)
    state_pool = nrm((ncd, DEC_BATCH, POOL_BUF, POOL_WIDTH))

    norm_mix = 1.0 + nrm((DEPTH, D_MODEL), 0.02)
    norm_ffn = 1.0 + nrm((DEPTH, D_MODEL), 0.02)
    norm_final = 1.0 + nrm((D_MODEL,), 0.02)

    w_in_ab = nrm((nab, D_MODEL, AB_IN), D_MODEL ** -0.5)
    w_out_ab = nrm((nab, MIX_WIDTH, D_MODEL), MIX_WIDTH ** -0.5)
    s5_a_re = -0.5 + nrm((nab, S5_GROUPS, S5_STATE), 0.01)
    s5_a_im = math.pi * jnp.arange(S5_STATE, dtype=f32) + nrm((nab, S5_GROUPS, S5_STATE), 0.01)
    s5_log_dt = jax.random.uniform(nk(), (nab, S5_GROUPS), f32, math.log(1e-3), math.log(1e-1))
    s5_b_re = nrm((nab, S5_GROUPS, S5_STATE, S5_GROUP), (2 * S5_GROUP) ** -0.5)
    s5_b_im = nrm((nab, S5_GROUPS, S5_STATE, S5_GROUP), (2 * S5_GROUP) ** -0.5)
    s5_c_re = nrm((nab, S5_GROUPS, S5_GROUP, S5_STATE), (2 * S5_STATE) ** -0.5)
    s5_c_im = nrm((nab, S5_GROUPS, S5_GROUP, S5_STATE), (2 * S5_STATE) ** -0.5)
    s5_d = nrm((nab, S5_WIDTH))
    s5_w_glu = nrm((nab, S5_WIDTH, S5_WIDTH), S5_WIDTH ** -0.5)
    s5_b_glu = nrm((nab, S5_WIDTH), 0.01)
    diff_lq1 = nrm((nab, DIFF_HEAD_DIM), 0.1)
    diff_lk1 = nrm((nab, DIFF_HEAD_DIM), 0.1)
    diff_lq2 = nrm((nab, DIFF_HEAD_DIM), 0.1)
    diff_lk2 = nrm((nab, DIFF_HEAD_DIM), 0.1)
    diff_subln = 1.0 + nrm((nab, 2 * DIFF_HEAD_DIM), 0.02)

    w_in_cd = nrm((ncd, D_MODEL, CD_IN), D_MODEL ** -0.5)
    w_out_cd = nrm((ncd, MIX_WIDTH, D_MODEL), MIX_WIDTH ** -0.5)
    conv_w = nrm((ncd, CONV_WIDTH, LRU_WIDTH), CONV_WIDTH ** -0.5)
    conv_b = nrm((ncd, LRU_WIDTH), 0.01)
    lru_wa = nrm((ncd, LRU_BLOCKS, LRU_BLOCK, LRU_BLOCK), LRU_BLOCK ** -0.5)
    lru_ba = nrm((ncd, LRU_WIDTH), 0.01)
    lru_wx = nrm((ncd, LRU_BLOCKS, LRU_BLOCK, LRU_BLOCK), LRU_BLOCK ** -0.5)
    lru_bx = nrm((ncd, LRU_WIDTH), 0.01)
    lru_u = jax.random.uniform(nk(), (ncd, LRU_WIDTH), f32, 0.9, 0.999)
    lru_sig = lru_u ** (1.0 / LRU_C)
    lru_lambda = jnp.log(lru_sig) - jnp.log1p(-lru_sig)
    pool_w = nrm((ncd, len(POOL_WINDOWS), POOL_GROUP, POOL_GROUP), POOL_GROUP ** -0.5)
    pool_scale = 1.0 + nrm((ncd, POOL_WIDTH), 0.1)

    ffn_w_gate = nrm((DEPTH, D_MODEL, FFN_HIDDEN), D_MODEL ** -0.5)
    ffn_w_up = nrm((DEPTH, D_MODEL, FFN_HIDDEN), D_MODEL ** -0.5)
    ffn_w_down = nrm((DEPTH, FFN_HIDDEN, D_MODEL), FFN_HIDDEN ** -0.5)

    return {'x_prompt': x_prompt, 'x_sample': x_sample, 'cache_k': cache_k, 'cache_v': cache_v,
            'page_table': page_table, 'state_s5_re': state_s5_re, 'state_s5_im': state_s5_im,
            'state_conv': state_conv, 'state_lru': state_lru, 'state_pool': state_pool,
            'norm_mix': norm_mix, 'norm_ffn': norm_ffn, 'norm_final': norm_final,
            'w_in_ab': w_in_ab, 'w_out_ab': w_out_ab, 's5_a_re': s5_a_re, 's5_a_im': s5_a_im,
            's5_log_dt': s5_log_dt, 's5_b_re': s5_b_re, 's5_b_im': s5_b_im, 's5_c_re': s5_c_re,
            's5_c_im': s5_c_im, 's5_d': s5_d, 's5_w_glu': s5_w_glu, 's5_b_glu': s5_b_glu,
            'diff_lq1': diff_lq1, 'diff_lk1': diff_lk1, 'diff_lq2': diff_lq2, 'diff_lk2': diff_lk2,
            'diff_subln': diff_subln, 'w_in_cd': w_in_cd, 'w_out_cd': w_out_cd, 'conv_w': conv_w,
            'conv_b': conv_b, 'lru_wa': lru_wa, 'lru_ba': lru_ba, 'lru_wx': lru_wx, 'lru_bx': lru_bx,
            'lru_lambda': lru_lambda, 'pool_w': pool_w, 'pool_scale': pool_scale,
            'ffn_w_gate': ffn_w_gate, 'ffn_w_up': ffn_w_up, 'ffn_w_down': ffn_w_down}


def reference(x_prompt, x_sample, cache_k, cache_v, page_table, state_s5_re, state_s5_im, state_conv,
              state_lru, state_pool, norm_mix, norm_ffn, norm_final, w_in_ab, w_out_ab, s5_a_re, s5_a_im,
              s5_log_dt, s5_b_re, s5_b_im, s5_c_re, s5_c_im, s5_d, s5_w_glu, s5_b_glu, diff_lq1, diff_lk1,
              diff_lq2, diff_lk2, diff_subln, w_in_cd, w_out_cd, conv_w, conv_b, lru_wa, lru_ba, lru_wx,
              lru_bx, lru_lambda, pool_w, pool_scale, ffn_w_gate, ffn_w_up, ffn_w_down):
    n_seq_dec, n_pages = page_table.shape
    past_len = n_pages * PAGE_SIZE

    def run(x, pos, sample):
        bsz = x.shape[0]
        new = {'k': [], 'v': [], 're': [], 'im': [], 'conv': [], 'lru': [], 'pool': []}
        for l in range(DEPTH):
            j = l // 2
            hn = rmsnorm(x, norm_mix[l])
            if l % 2 == 0:
                if sample:
                    k_past = cache_k[j][page_table].reshape(n_seq_dec, past_len, 2 * DIFF_HEADS, DIFF_HEAD_DIM)
                    v_past = cache_v[j][page_table].reshape(n_seq_dec, past_len, DIFF_HEADS, 2 * DIFF_HEAD_DIM)
                    past = (k_past, v_past, jnp.arange(past_len, dtype=jnp.int32))
                    h0_re, h0_im = state_s5_re[j], state_s5_im[j]
                else:
                    past = None
                    h0_re = jnp.zeros((bsz, S5_GROUPS, S5_STATE), jnp.float32)
                    h0_im = jnp.zeros((bsz, S5_GROUPS, S5_STATE), jnp.float32)
                lam_init = 0.8 - 0.6 * math.exp(-0.3 * l)
                mix, k_new, v_new, h_re, h_im = ab_mixer(
                    hn, pos, past, h0_re, h0_im, w_in_ab[j], w_out_ab[j], s5_a_re[j], s5_a_im[j],
                    s5_log_dt[j], s5_b_re[j], s5_b_im[j], s5_c_re[j], s5_c_im[j], s5_d[j], s5_w_glu[j],
                    s5_b_glu[j], diff_lq1[j], diff_lk1[j], diff_lq2[j], diff_lk2[j], diff_subln[j], lam_init)
                new['k'].append(k_new)
                new['v'].append(v_new)
                new['re'].append(h_re)
                new['im'].append(h_im)
            else:
                if sample:
                    conv_buf, lru_h0, pool_buf = state_conv[j], state_lru[j], state_pool[j]
                else:
                    conv_buf = jnp.zeros((bsz, CONV_WIDTH - 1, LRU_WIDTH), x.dtype)
                    lru_h0 = jnp.zeros((bsz, LRU_WIDTH), jnp.float32)
                    pool_buf = jnp.zeros((bsz, POOL_BUF, POOL_WIDTH), x.dtype)
                mix, c_new, h_new, p_new = cd_mixer(
                    hn, pos, conv_buf, lru_h0, pool_buf, w_in_cd[j], w_out_cd[j], conv_w[j], conv_b[j],
                    lru_wa[j], lru_ba[j], lru_wx[j], lru_bx[j], lru_lambda[j], pool_w[j], pool_scale[j])
                new['conv'].append(c_new)
                new['lru'].append(h_new)
                new['pool'].append(p_new)
            x = x + mix
            x = x + swiglu(rmsnorm(x, norm_ffn[l]), ffn_w_gate[l], ffn_w_up[l], ffn_w_down[l])
        y = rmsnorm(x, norm_final)
        return y, {name: jnp.stack(vals) for name, vals in new.items()}

    pos_p = jnp.arange(x_prompt.shape[1], dtype=jnp.int32)
    pos_s = past_len + jnp.arange(x_sample.shape[1], dtype=jnp.int32)
    y_prompt, sp = run(x_prompt, pos_p, False)
    y_sample, ss = run(x_sample, pos_s, True)
    return (y_prompt, y_sample, sp['k'], sp['v'], ss['k'], ss['v'], sp['re'], sp['im'], ss['re'], ss['im'],
            sp['conv'], ss['conv'], sp['lru'], ss['lru'], sp['pool'], ss['pool'])
```

```python
import math
import contextlib
import numpy as np
import concourse.bass as bass
import concourse.mybir as mybir
from concourse.bass_utils import run_bass_kernel_spmd

F32 = mybir.dt.float32
BF16 = mybir.dt.bfloat16
I32 = mybir.dt.int32
AF = mybir.ActivationFunctionType
ALU = mybir.AluOpType
AX = mybir.AxisListType

ENGS = ['pe', 'act', 'dve', 'pool', 'sp']
NDSEM = 94
DQ = {'sp': (0, 47), 'pool': (47, 47), 'act': (0, 0)}

D = 1024
SEQ = 8192
NT = SEQ // 128
NST = SEQ // 512
FFN = 2816
PAST = 16384
NPAGE = 128
NPOOL = 5120
TWO_PI = 2.0 * math.pi
MAGIC = 12582912.0


class Prog:
    def __init__(self, nc):
        self.nc = nc
        self.ops = {e: [] for e in ENGS}
        self.cnt = {e: 0 for e in ENGS}
        self.known = {e: {} for e in ENGS}
        self.lastw = {}
        self.readers = {}
        self.dma_k = {e: 0 for e in ENGS}
        self.dsem_cnt = [0] * NDSEM

    def _deps(self, reads, writes):
        deps = []
        for t in list(reads) + list(writes):
            d = self.lastw.get(t)
            if d is not None:
                deps.append(d)
        for t in writes:
            deps.extend(self.readers.get(t, []))
        return deps

    def _record(self, dep, reads, writes):
        for t in writes:
            self.lastw[t] = dep
            self.readers[t] = []
        for t in reads:
            if t in writes:
                continue
            self.readers.setdefault(t, []).append(dep)

    def _waits(self, eng, deps, is_dma=False):
        need = {}
        for kind, key, val in deps:
            if kind == 'c' and key == eng and not is_dma and eng == 'pe':
                continue
            k = (kind, key)
            if self.known[eng].get(k, 0) >= val:
                continue
            need[k] = max(need.get(k, 0), val)
        for k, v in need.items():
            self.known[eng][k] = v
        return list(need.items())

    def op(self, eng, fn, reads=(), writes=()):
        deps = self._deps(reads, writes)
        waits = self._waits(eng, deps)
        self.cnt[eng] += 1
        dep = ('c', eng, self.cnt[eng])
        self.ops[eng].append((fn, waits, ('c', eng)))
        self._record(dep, reads, writes)

    def dma(self, eng, fn, reads=(), writes=()):
        deps = self._deps(reads, writes)
        base, n = DQ[eng]
        s = base + self.dma_k[eng] % n
        self.dma_k[eng] += 1
        if self.dsem_cnt[s] > 0:
            deps = list(deps) + [('d', s, self.dsem_cnt[s])]
        waits = self._waits(eng, deps, is_dma=True)
        self.dsem_cnt[s] += 16
        dep = ('d', s, self.dsem_cnt[s])
        self.ops[eng].append((fn, waits, ('d', s)))
        self._record(dep, reads, writes)

    def emit(self):
        nc = self.nc
        with contextlib.ExitStack() as st:
            csem = {e: st.enter_context(nc.semaphore('c_' + e)) for e in ENGS}
            dsem = [st.enter_context(nc.semaphore('d_%d' % i)) for i in range(NDSEM)]
            block = st.enter_context(nc.Block())

            def semof(k):
                return csem[k[1]] if k[0] == 'c' else dsem[k[1]]
            final_waits = [(('d', i), v) for i, v in enumerate(self.dsem_cnt) if v > 0]
            final_waits += [(('c', e), v) for e, v in self.cnt.items() if v > 0 and e != 'sp']

            def run(engname, eo):
                for fn, waits, inc in self.ops[engname]:
                    for k, v in waits:
                        eo.wait_ge(semof(k), v)
                    ins = fn(eo)
                    if inc[0] == 'c':
                        ins.then_inc(csem[inc[1]], 1)
                    else:
                        ins.then_inc(dsem[inc[1]], 16)
                if engname == 'sp':
                    for k, v in final_waits:
                        eo.wait_ge(semof(k), v)

            @block.tensor
            def _(e):
                run('pe', e)

            @block.scalar
            def _(e):
                run('act', e)

            @block.vector
            def _(e):
                run('dve', e)

            @block.gpsimd
            def _(e):
                run('pool', e)

            @block.sync
            def _(e):
                run('sp', e)


class RPool:
    def __init__(self, name, tensors):
        self.name, self.t, self.i = name, tensors, 0

    def get(self):
        i = self.i % len(self.t)
        self.i += 1
        return self.t[i], (self.name, i)


def build(n_st=NST, do_sample=True, debug=False, npool=NPOOL):
    nc = bass.Bass("TRN2", target_bir_lowering=False)
    dt_in = {}

    def din(name, shape, dt=F32):
        dt_in[name] = (shape, dt)
        return nc.dram_tensor(name, list(shape), dt, kind="ExternalInput").ap()

    def dout(name, shape):
        return nc.dram_tensor(name, list(shape), F32, kind="ExternalOutput").ap()

    xp = din("xp", [SEQ, D])
    xs = din("xs", [4, D])
    ck = din("ck", [npool, 128 * 512]) if do_sample else None
    cv = din("cv", [npool, 128 * 512]) if do_sample else None
    pt = din("pt", [128, 4], I32)
    st_re = din("st_re", [4, 32, 64])
    st_im = din("st_im", [4, 32, 64])
    st_conv = din("st_conv", [4, 3, 512])
    st_lru = din("st_lru", [4, 512])
    st_pool = din("st_pool", [4, 15, 512])
    norm_mix = din("norm_mix", [2, D])
    norm_ffn = din("norm_ffn", [2, D])
    norm_final = din("norm_final", [D])
    w_in_ab = din("w_in_ab", [D, 2048])
    w_out_ab = din("w_out_ab", [D, D])
    a_re = din("s5_a_re", [32, 64])
    a_im = din("s5_a_im", [32, 64])
    log_dt = din("s5_log_dt", [32])
    b_re = din("s5_b_re", [32, 64, 16])
    b_im = din("s5_b_im", [32, 64, 16])
    c_re = din("s5_c_re", [32, 16, 64])
    c_im = din("s5_c_im", [32, 16, 64])
    s5_d = din("s5_d", [512])
    w_glu = din("s5_w_glu", [512, 512])
    b_glu = din("s5_b_glu", [512])
    lq1 = din("diff_lq1", [64])
    lk1 = din("diff_lk1", [64])
    lq2 = din("diff_lq2", [64])
    lk2 = din("diff_lk2", [64])
    subln = din("diff_subln", [128])
    w_in_cd = din("w_in_cd", [D, 1536])
    w_out_cd = din("w_out_cd", [D, D])
    conv_w = din("conv_w", [4, 512])
    conv_b = din("conv_b", [512])
    lru_wa = din("lru_wa", [8, 64, 64])
    lru_ba = din("lru_ba", [512])
    lru_wx = din("lru_wx", [8, 64, 64])
    lru_bx = din("lru_bx", [512])
    lru_lam = din("lru_lambda", [512])
    pool_w = din("pool_w", [4, 128, 128])
    pool_scale = din("pool_scale", [512])
    wg = din("ffn_w_gate", [2, D, FFN])
    wu = din("ffn_w_up", [2, D, FFN])
    wd = din("ffn_w_down", [2, FFN, D])
    c_ident = din("c_ident", [128, 128])
    c_trie = din("c_trie", [128, 129])
    c_sel2 = din("c_sel2", [128, 128])
    c_gmask = din("c_gmask", [128, 8])
    c_iota = din("c_iota", [128, 640])
    c_part = din("c_part", [128, 1])
    c_bmask = din("c_bmask", [8, 512])
    c_oh84 = din("c_oh84", [8, 4, 4])
    c_oh44 = din("c_oh44", [4, 4, 128])
    c_oh14 = din("c_oh14", [1, 4, 4])
    c_alt = din("c_alt", [8, 1])

    y_p = dout("y_p", [SEQ, D])
    y_s = dout("y_s", [4, D])
    k_p = dout("k_p", [SEQ, 512])
    v_p = dout("v_p", [SEQ, 512])
    k_s = dout("k_s", [4, 512])
    v_s = dout("v_s", [4, 512])
    re_p = dout("re_p", [32, 64])
    im_p = dout("im_p", [32, 64])
    re_s = dout("re_s", [4, 32, 64])
    im_s = dout("im_s", [4, 32, 64])
    conv_p = dout("conv_p", [3, 512])
    conv_s = dout("conv_s", [4, 3, 512])
    lru_p = dout("lru_p", [512])
    lru_s = dout("lru_s", [4, 512])
    pool_p = dout("pool_p", [15, 512])
    pool_s = dout("pool_s", [4, 15, 512])
    kt_scr = nc.dram_tensor("kt_scr", [4, 128, SEQ], BF16, kind="Internal").ap()
    if debug:
        dbgR = dout("dbgR", [4, 512, D])
        dbgA = nc.dram_tensor("dbgA", [2, 8, 128, 512], BF16, kind="ExternalOutput").ap()
        dbgS = nc.dram_tensor("dbgS", [8, 128, 4], BF16, kind="ExternalOutput").ap()
        dbgRs = dout("dbgRs", [2, 4, D])

    P = Prog(nc)
    es = contextlib.ExitStack()
    uid = [0]

    def sb(shape, dt=F32, name=None):
        uid[0] += 1
        return es.enter_context(nc.sbuf_tensor(name or ("t%d" % uid[0]), list(shape), dt))

    def psum(shape, dt=F32):
        uid[0] += 1
        return es.enter_context(nc.psum_tensor("p%d" % uid[0], list(shape), dt))

    with es:
        fpool = RPool('f', [sb([128, 512]) for _ in range(10)])
        hpool = RPool('h', [sb([128, 512], BF16) for _ in range(8)])
        xpool = RPool('x', [sb([128, 512], BF16) for _ in range(4)])
        zpool = RPool('z', [sb([128, 128], BF16) for _ in range(4)])
        kpool = RPool('k', [sb([128, 512], BF16) for _ in range(2)])
        wpool = RPool('w', [sb([128, 8, 512], BF16) for _ in range(3)])
        spool = RPool('s', [sb([128, 8]) for _ in range(12)])
        pspool = RPool('ps', [psum([128, 512]) for _ in range(4)])
        pbpool = RPool('pb', [psum([128, 1024], BF16) for _ in range(1)])
        psO = [psum([128, 512]) for _ in range(2)]
        psL = psum([128, 512])

        R = sb([128, 4, D])
        aT = sb([128, 8, 512], BF16)
        QT = sb([128, 4, 512], BF16)
        junk = sb([128, D], BF16)
        gbuf = sb([128, D])

        def aTk(n):
            return ('aT', n)
        aT_all = [aTk(n) for n in range(4)]

        def tt(eng, out, a, b, op, reads, writes):
            P.op(eng, lambda e: e.tensor_tensor(out=out, in0=a, in1=b, op=op), reads, writes)

        def ts(eng, out, a, s1, op0, s2=None, op1=None, reads=(), writes=()):
            if op1 is None:
                P.op(eng, lambda e: e.tensor_scalar(out=out, in0=a, scalar1=s1, scalar2=None, op0=op0), reads, writes)
            else:
                P.op(eng, lambda e: e.tensor_scalar(out=out, in0=a, scalar1=s1, scalar2=s2, op0=op0, op1=op1), reads, writes)

        def stt(out, a, s, b, op0, op1, reads, writes):
            P.op('dve', lambda e: e.scalar_tensor_tensor(out=out, in0=a, scalar=s, in1=b, op0=op0, op1=op1), reads, writes)

        def act(out, a, func, reads, writes, scale=None, bias=None, accum=None):
            kw = {}
            if scale is not None:
                kw['scale'] = scale
            if bias is not None:
                kw['bias'] = bias
            if accum is not None:
                kw['accum_out'] = accum
            P.op('act', lambda e: e.activation(out=out, in_=a, func=func, **kw), reads, writes)

        def cp(eng, out, a, reads, writes):
            if eng == 'act':
                P.op('act', lambda e: e.activation(out=out, in_=a, func=AF.Copy), reads, writes)
            else:
                P.op(eng, lambda e: e.tensor_copy(out=out, in_=a), reads, writes)

        def recip(out, a, reads, writes):
            P.op('dve', lambda e: e.reciprocal(out=out, in_=a), reads, writes)

        def mm(out, lhsT, rhs, start, stop, reads, writes):
            P.op('pe', lambda e: e.matmul(out, lhsT=lhsT, rhs=rhs, start=start, stop=stop), reads, writes)

        def tr(out, in_, ident, reads, writes):
            P.op('pe', lambda e: e.transpose(out=out, in_=in_, identity=ident), reads, writes)

        def ld(eng, out, in_, writes, reads=()):
            P.dma(eng, lambda e: e.dma_start(out=out, in_=in_), reads=reads, writes=writes)

        def st_(out, in_, reads, writes=()):
            P.dma('sp', lambda e: e.dma_start(out=out, in_=in_), reads=reads, writes=writes)

        identf = sb([128, 128])
        identb = sb([128, 128], BF16)
        trie = sb([128, 129], BF16)
        sel2 = sb([128, 128])
        gmask = sb([128, 8])
        iota = sb([128, 640])
        partc = sb([128, 1])
        bmask = sb([8, 512])
        oh84 = sb([8, 4, 4])
        oh44 = sb([4, 4, 128])
        oh14 = sb([1, 4, 4])
        altc = sb([8, 1])
        ld('sp', identf[:], c_ident[:, :], ['identf'])
        ld('pool', identb[:], c_ident[:, :], ['identb'])
        ld('pool', trie[:], c_trie[:, :], ['trie'])
        ld('sp', sel2[:], c_sel2[:, :], ['sel2'])
        ld('sp', gmask[:], c_gmask[:, :], ['gmask'])
        ld('sp', iota[:], c_iota[:, :], ['iota'])
        ld('sp', partc[:], c_part[:, :], ['partc'])
        ld('sp', bmask[:], c_bmask[:, :], ['bmask'])
        ld('sp', oh84[:], c_oh84[:, :, :], ['oh84'])
        ld('sp', oh44[:], c_oh44[:, :, :], ['oh44'])
        ld('sp', oh14[:], c_oh14[:, :, :], ['oh14'])
        ld('sp', altc[:], c_alt[:, :], ['altc'])

        def bload(dst, src1d, tok, rows=128):
            ld('sp', dst, src1d.partition_broadcast(rows), [tok])

        gsrc = {'mix0': norm_mix[0, :], 'mix1': norm_mix[1, :], 'ffn0': norm_ffn[0, :], 'ffn1': norm_ffn[1, :], 'fin': norm_final[:]}

        def load_gain(key):
            bload(gbuf[:], gsrc[key], 'gbuf')

        def cols_from_rows(dst, src2d, nrows, reads_tok, wtok):
            t, kt = fpool.get()
            ld('sp', t[0:nrows, 0:128], src2d, [kt])
            ps, kps = pspool.get()
            tr(ps[:, 0:nrows], t[0:nrows, 0:128], identf[0:nrows, 0:nrows], [kt, 'identf'], [kps])
            cp('dve', dst, ps[:, 0:nrows], [kps], [wtok])

        def rows_out(dst2d, src, nrows, reads):
            ps, kps = pspool.get()
            tr(ps[0:nrows, 0:128], src, identf[:], list(reads) + ['identf'], [kps])
            t, kt = fpool.get()
            cp('dve', t[0:nrows, 0:128], ps[0:nrows, 0:128], [kps], [kt])
            st_(dst2d, t[0:nrows, 0:128], [kt])

        def range_reduce_sin(out, x, shift, rows, cols, reads, writes):
            t1, k1 = fpool.get()
            t2, k2 = fpool.get()
            a = t1[0:rows, 0:cols]
            b = t2[0:rows, 0:cols]
            ts('dve', a, x, shift, ALU.add, reads=reads, writes=[k1])
            ts('dve', b, a, 1.0 / TWO_PI, ALU.mult, MAGIC, ALU.add, reads=[k1], writes=[k2])
            ts('dve', b, b, -MAGIC, ALU.add, -TWO_PI, ALU.mult, reads=[k2], writes=[k2])
            tt('dve', a, a, b, ALU.add, [k1, k2], [k1])
            ts('dve', a, a, 3.1415925, ALU.min, -3.1415925, ALU.max, reads=[k1], writes=[k1])
            act(out, a, AF.Sin, [k1], writes)

        def gelu_tanh(out, x, rows, cols, reads, writes):
            t1, k1 = fpool.get()
            a = t1[0:rows, 0:cols]
            tt('dve', a, x, x, ALU.mult, reads, [k1])
            ts('dve', a, a, 0.044715, ALU.mult, 1.0, ALU.add, reads=[k1], writes=[k1])
            tt('dve', a, a, x, ALU.mult, list(reads) + [k1], [k1])
            act(a, a, AF.Sigmoid, [k1], [k1], scale=1.5957691216057308)
            tt('dve', out, a, x, ALU.mult, list(reads) + [k1], writes)

        lam_init0 = 0.8 - 0.6 * math.exp(-0.3 * 0)
        lamt = sb([128, 4])
        lq_t, klq_t = fpool.get()
        lq = lq_t[:].rearrange("p (a b) -> p a b", a=8)
        for i, src in enumerate([lq1, lk1, lq2, lk2]):
            P.dma('sp', lambda e, i=i, src=src: e.dma_start(out=lq[:, i, :], in_=src[:].partition_broadcast(128)), reads=[klq_t], writes=[klq_t])
        ltmp_t, kltmp_t = fpool.get()
        ltmp = ltmp_t[:].rearrange("p (a b) -> p a b", a=8)[:, 0:2, :]
        tt('dve', ltmp[:, 0, :], lq[:, 0, :], lq[:, 1, :], ALU.mult, [klq_t, kltmp_t], [kltmp_t])
        tt('dve', ltmp[:, 1, :], lq[:, 2, :], lq[:, 3, :], ALU.mult, [klq_t, kltmp_t], [kltmp_t])
        lsum = sb([128, 2])
        P.op('dve', lambda e: e.tensor_reduce(out=lsum[:], in_=ltmp, axis=AX.X, op=ALU.add), [kltmp_t], ['lsum'])
        act(lsum[:], lsum[:], AF.Exp, ['lsum'], ['lsum'])
        tt('dve', lamt[:, 0:1], lsum[:, 0:1], lsum[:, 1:2], ALU.subtract, ['lsum'], ['lamt'])
        ts('dve', lamt[:, 0:1], lamt[:, 0:1], lam_init0, ALU.add, reads=['lamt'], writes=['lamt'])
        ts('dve', lamt[:, 1:2], lamt[:, 0:1], -1.0, ALU.mult, reads=['lamt'], writes=['lamt'])
        ts('dve', lamt[:, 2:3], lamt[:, 1:2], -1.0, ALU.add, reads=['lamt'], writes=['lamt'])
        sublnB = sb([128, 128])
        bload(sublnB[:], subln[:], 'sublnB')
        ts('dve', sublnB[:], sublnB[:], 1.0 - lam_init0, ALU.mult, reads=['sublnB'], writes=['sublnB'])

        invB = sb([128, 32])
        act(invB[:], iota[:, 0:32], AF.Exp, ['iota'], ['invB'], scale=-math.log(10000.0) / 32.0)

        thg = sb([128, 32])
        rhg = sb([128, 32])
        dtg = sb([128, 32])
        are_g = sb([128, 32])
        aim_g = sb([128, 32])
        for (src, dst, nm) in ((a_re, are_g, 'are_g'), (a_im, aim_g, 'aim_g')):
            t, kt = fpool.get()
            ld('sp', t[0:32, 0:64], src[:, :], [kt])
            ld('sp', t[0:32, 64:128], src[:, :], [kt], reads=[kt])
            ps, kps = pspool.get()
            tr(ps[:, 0:32], t[0:32, 0:128], identf[0:32, 0:32], [kt, 'identf'], [kps])
            cp('dve', dst[:], ps[:, 0:32], [kps], [nm])
        bload(dtg[:], log_dt[:], 'dtg')
        act(dtg[:], dtg[:], AF.Exp, ['dtg'], ['dtg'])
        tt('dve', thg[:], aim_g[:], dtg[:], ALU.mult, ['aim_g', 'dtg'], ['thg'])
        tt('dve', rhg[:], are_g[:], dtg[:], ALU.mult, ['are_g', 'dtg'], ['rhg'])
        VR2 = sb([128, 32, 128], BF16)
        VI2 = sb([128, 32, 128], BF16)
        Vc = sb([128, 4, 32])
        for g0 in range(0, 32, 4):
            ang, ka = fpool.get()
            mag, km = fpool.get()
            a3 = ang[:].rearrange("p (g t) -> p g t", g=4)
            m3 = mag[:].rearrange("p (g t) -> p g t", g=4)
            io3 = iota[:, 0:128].unsqueeze(1).broadcast_to([128, 4, 128])
            tt('dve', a3, io3, thg[:, g0:g0 + 4].unsqueeze(2).broadcast_to([128, 4, 128]), ALU.mult, ['iota', 'thg'], [ka])
            tt('dve', m3, io3, rhg[:, g0:g0 + 4].unsqueeze(2).broadcast_to([128, 4, 128]), ALU.mult, ['iota', 'rhg'], [km])
            act(mag[:], mag[:], AF.Exp, [km], [km])
            sn, ksn = fpool.get()
            cs, kcs = fpool.get()
            range_reduce_sin(sn[:], ang[:], 0.0, 128, 512, [ka], [ksn])
            range_reduce_sin(cs[:], ang[:], math.pi / 2, 128, 512, [ka], [kcs])
            tt('dve', VR2[:, g0:g0 + 4, :], cs[:].rearrange("p (g t) -> p g t", g=4), m3, ALU.mult, [kcs, km], [('VR2', g0)])
            tt('dve', VI2[:, g0:g0 + 4, :], sn[:].rearrange("p (g t) -> p g t", g=4), m3, ALU.mult, [ksn, km], [('VI2', g0)])

        def cplx_pow(tval, dre, dim, wre, wim):
            ang, kang = fpool.get()
            ts('dve', ang[:, 0:32], thg[:], tval, ALU.mult, reads=['thg'], writes=[kang])
            mg, kmg = fpool.get()
            act(mg[:, 0:32], rhg[:], AF.Exp, ['rhg'], [kmg], scale=tval)
            sn, ksn = fpool.get()
            cs, kcs = fpool.get()
            range_reduce_sin(sn[:, 0:32], ang[:, 0:32], 0.0, 128, 32, [kang], [ksn])
            range_reduce_sin(cs[:, 0:32], ang[:, 0:32], math.pi / 2, 128, 32, [kang], [kcs])
            tt('dve', dre, cs[:, 0:32], mg[:, 0:32], ALU.mult, [kcs, kmg], [wre])
            tt('dve', dim, sn[:, 0:32], mg[:, 0:32], ALU.mult, [ksn, kmg], [wim])
        cplx_pow(127.0, Vc[:, 0, :], Vc[:, 1, :], ('Vc', 0), ('Vc', 1))
        cplx_pow(128.0, Vc[:, 2, :], Vc[:, 3, :], ('Vc', 2), ('Vc', 3))
        VcT = [('Vc', i) for i in range(4)]
        lb = sb([128, 2, 32])
        cplx_pow(1.0, lb[:, 0, :], lb[:, 1, :], 'lb0', 'lb1')
        fre = sb([128, 32])
        fim = sb([128, 32])
        if True:
            nre, knre = fpool.get()
            ts('dve', nre[:, 0:32], lb[:, 0, :], -1.0, ALU.add, reads=['lb0'], writes=[knre])
            den, kden = fpool.get()
            t1, kt1 = fpool.get()
            t2, kt2 = fpool.get()
            tt('dve', den[:, 0:32], are_g[:], are_g[:], ALU.mult, ['are_g'], [kden])
            tt('dve', t1[:, 0:32], aim_g[:], aim_g[:], ALU.mult, ['aim_g'], [kt1])
            tt('dve', den[:, 0:32], den[:, 0:32], t1[:, 0:32], ALU.add, [kden, kt1], [kden])
            recip(den[:, 0:32], den[:, 0:32], [kden], [kden])
            tt('dve', t1[:, 0:32], nre[:, 0:32], are_g[:], ALU.mult, [knre, 'are_g'], [kt1])
            tt('dve', t2[:, 0:32], lb[:, 1, :], aim_g[:], ALU.mult, ['lb1', 'aim_g'], [kt2])
            tt('dve', t1[:, 0:32], t1[:, 0:32], t2[:, 0:32], ALU.add, [kt1, kt2], [kt1])
            tt('dve', fre[:], t1[:, 0:32], den[:, 0:32], ALU.mult, [kt1, kden], ['fre'])
            tt('dve', t1[:, 0:32], lb[:, 1, :], are_g[:], ALU.mult, ['lb1', 'are_g'], [kt1])
            tt('dve', t2[:, 0:32], nre[:, 0:32], aim_g[:], ALU.mult, [knre, 'aim_g'], [kt2])
            tt('dve', t1[:, 0:32], t1[:, 0:32], t2[:, 0:32], ALU.subtract, [kt1, kt2], [kt1])
            tt('dve', fim[:], t1[:, 0:32], den[:, 0:32], ALU.mult, [kt1, kden], ['fim'])
        bnr, kbnr = fpool.get()
        bni, kbni = fpool.get()
        ld('sp', bnr[0:64, :].rearrange("p (g c) -> p g c", g=32), b_re.rearrange("g p c -> p g c"), [kbnr])
        ld('sp', bni[0:64, :].rearrange("p (g c) -> p g c", g=32), b_im.rearrange("g p c -> p g c"), [kbni])
        bst2, kbst2 = fpool.get()
        t1, kt1 = fpool.get()
        t2, kt2 = fpool.get()

        def f3(ap64):
            return ap64.unsqueeze(2).broadcast_to([64, 32, 16])

        def v3(ap):
            return ap.rearrange("p (g c) -> p g c", g=32)
        tt('dve', v3(t1[0:64, :]), v3(bnr[0:64, :]), f3(fre[0:64, :]), ALU.mult, [kbnr, 'fre'], [kt1])
        tt('dve', v3(t2[0:64, :]), v3(bni[0:64, :]), f3(fim[0:64, :]), ALU.mult, [kbni, 'fim'], [kt2])
        tt('dve', bst2[0:64, :], t1[0:64, :], t2[0:64, :], ALU.subtract, [kt1, kt2], [kbst2])
        tt('dve', v3(t1[0:64, :]), v3(bni[0:64, :]), f3(fre[0:64, :]), ALU.mult, [kbni, 'fre'], [kt1])
        tt('dve', v3(t2[0:64, :]), v3(bnr[0:64, :]), f3(fim[0:64, :]), ALU.mult, [kbnr, 'fim'], [kt2])
        tt('dve', t1[0:64, :], t1[0:64, :], t2[0:64, :], ALU.add, [kt1, kt2], [kt1])
        st_(bst2[64:128, :], t1[0:64, :], [kt1, kbst2], [kbst2])
        bst, kbst = hpool.get()
        cp('dve', bst[:], bst2[:], [kbst2], [kbst])
        BB = sb([128, 4, 8, 128], BF16)
        BBs = sb([128, 4, 8, 128], BF16)
        for q in range(4):
            pb, kpb = pbpool.get()
            tr(pb[:, 0:128], bst[:, 128 * q:128 * q + 128], identb[:], [kbst, 'identb'], [kpb])
            tq, ktq = hpool.get()
            cp('dve', tq[:, 0:128], pb[:, 0:128], [kpb], [ktq])
            tt('dve', BB[:, q, :, :], tq[:, 0:128].unsqueeze(1).broadcast_to([128, 8, 128]),
               gmask[:].unsqueeze(2).broadcast_to([128, 8, 128]), ALU.mult, [ktq, 'gmask'], [('BB', q)])
            cp('dve', BBs[:, q, :, 0:64], BB[:, q, :, 64:128], [('BB', q)], [('BBs0', q)])
            cp('dve', BBs[:, q, :, 64:128], BB[:, q, :, 0:64], [('BB', q)], [('BBs1', q)])
        BBT = [('BB', q) for q in range(4)]
        BBsT = [('BBs0', q) for q in range(4)] + [('BBs1', q) for q in range(4)]
        WA = sb([128, 32, 128], BF16)
        WB = sb([128, 32, 128], BF16)
        dtrow = sb([128, 32])
        bload(dtrow[:], log_dt[:], 'dtrow')
        act(dtrow[:], dtrow[:], AF.Exp, ['dtrow'], ['dtrow'])
        for g0 in range(0, 32, 8):
            ang, ka = fpool.get()
            mag, km = fpool.get()
            bload(ang[:], a_im[g0:g0 + 8, :].rearrange("g p -> (g p)"), ka)
            bload(mag[:], a_re[g0:g0 + 8, :].rearrange("g p -> (g p)"), km)
            d3 = dtrow[:, g0:g0 + 8].unsqueeze(2).broadcast_to([128, 8, 64])
            tt('dve', ang[:].rearrange("p (g s) -> p g s", g=8), ang[:].rearrange("p (g s) -> p g s", g=8), d3, ALU.mult, [ka, 'dtrow'], [ka])
            tt('dve', mag[:].rearrange("p (g s) -> p g s", g=8), mag[:].rearrange("p (g s) -> p g s", g=8), d3, ALU.mult, [km, 'dtrow'], [km])
            ts('dve', ang[:], ang[:], partc[:, 0:1], ALU.mult, reads=[ka, 'partc'], writes=[ka])
            ts('dve', mag[:], mag[:], partc[:, 0:1], ALU.mult, reads=[km, 'partc'], writes=[km])
            act(mag[:], mag[:], AF.Exp, [km], [km], scale=-1.0)
            sn, ksn = fpool.get()
            cs, kcs = fpool.get()
            range_reduce_sin(sn[:], ang[:], 0.0, 128, 512, [ka], [ksn])
            range_reduce_sin(cs[:], ang[:], math.pi / 2, 128, 512, [ka], [kcs])
            tt('dve', cs[:], cs[:], mag[:], ALU.mult, [kcs, km], [kcs])
            tt('dve', sn[:], sn[:], mag[:], ALU.mult, [ksn, km], [ksn])
            c3 = cs[:].rearrange("p (g s) -> p g s", g=8)
            s3 = sn[:].rearrange("p (g s) -> p g s", g=8)
            cp('dve', WA[:, g0:g0 + 8, 0:64], c3, [kcs], [('WA0', g0)])
            cp('dve', WA[:, g0:g0 + 8, 64:128], c3, [kcs], [('WA1', g0)])
            cp('dve', WB[:, g0:g0 + 8, 0:64], s3, [ksn], [('WB0', g0)])
            ts('dve', WB[:, g0:g0 + 8, 64:128], s3, -1.0, ALU.mult, reads=[ksn], writes=[('WB1', g0)])
        WT = [(n, g0) for n in ('WA0', 'WA1', 'WB0', 'WB1') for g0 in range(0, 32, 8)]
        CA = sb([128, 32, 16], BF16)
        CB = sb([128, 32, 16], BF16)
        for b in range(4):
            for (first, second, dstc, sgn_top, tok) in ((c_re, c_im, CA, 1.0, 'CA'), (c_im, c_re, CB, -1.0, 'CB')):
                t, kt = fpool.get()
                ld('sp', t[:, 0:64], first[8 * b:8 * b + 8, :, :].rearrange("g c p -> (g c) p"), [kt])
                ld('sp', t[:, 64:128], second[8 * b:8 * b + 8, :, :].rearrange("g c p -> (g c) p"), [kt], reads=[kt])
                ps, kps = pspool.get()
                tr(ps[:, 0:128], t[:, 0:128], identf[:], [kt, 'identf'], [kps])
                ts('dve', dstc[0:64, 8 * b:8 * b + 8, :], ps[0:64, 0:128].rearrange("p (g c) -> p g c", g=8), sgn_top, ALU.mult, reads=[kps], writes=[(tok, b, 0)])
                ts('dve', dstc[64:128, 8 * b:8 * b + 8, :], ps[64:128, 0:128].rearrange("p (g c) -> p g c", g=8), -1.0, ALU.mult, reads=[kps], writes=[(tok, b, 1)])
        CT = [(tok, b, h) for tok in ('CA', 'CB') for b in range(4) for h in range(2)]
        dB = sb([128, 512])
        bload(dB[:], s5_d[:], 'dB')
        bgB = sb([128, 512])
        bload(bgB[:], b_glu[:], 'bgB')
        H0 = sb([128, 32])
        Hn = sb([128, 32])
        Zc = sb([128, 4, 32])
        P.op('dve', lambda e: e.memset(H0[:], 0.0), [], ['H0'])

        masks = sb([128, 4, 512], BF16)
        ones_b = sb([128, 512], BF16)
        ones_f = sb([128, 1])
        P.op('dve', lambda e: e.memset(ones_b[:], 1.0), [], ['ones_b'])
        P.op('dve', lambda e: e.memset(ones_f[:], 1.0), [], ['ones_f'])
        for i in range(4):
            P.op('pool', lambda e, i=i: e.affine_select(out=masks[:, i, :], in_=ones_b[:], pattern=[[1, 512]],
                                                     compare_op=ALU.is_ge, fill=0.0, base=-128 * i,
                                                     channel_multiplier=-1), ['ones_b'], [('mask', i)])
        E2 = sb([128, 2, 2], BF16)
        P.op('dve', lambda e: e.memset(E2[:], 0.0), [], ['E2'])
        P.op('dve', lambda e: e.memset(E2[:, 0, 0:1], 1.0), ['E2'], ['E2'])
        P.op('dve', lambda e: e.memset(E2[:, 1, 1:2], 1.0), ['E2'], ['E2'])

        def rstd_col(rows, xap, xtok, n, eps):
            s, ks = spool.get()
            act(junk[0:rows, 0:n], xap, AF.Square, [xtok], ['junk', ks], accum=s[0:rows, 0:1])
            ts('dve', s[0:rows, 0:1], s[0:rows, 0:1], 1.0 / n, ALU.mult, eps, ALU.add, reads=[ks], writes=[ks])
            act(s[0:rows, 0:1], s[0:rows, 0:1], AF.Sqrt, [ks], [ks])
            recip(s[0:rows, 0:1], s[0:rows, 0:1], [ks], [ks])
            return s, ks

        def rmsnorm_T(rows, xtile, xtok, col0, dtok):
            s, ks = rstd_col(rows, xtile, xtok, D, 1e-6)
            h0, kh0 = hpool.get()
            h1, kh1 = hpool.get()
            stt(h0[0:rows, :], xtile[:, 0:512], s[0:rows, 0:1], gbuf[0:rows, 0:512], ALU.mult, ALU.mult, [xtok, ks, 'gbuf'], [kh0])
            stt(h1[0:rows, :], xtile[:, 512:1024], s[0:rows, 0:1], gbuf[0:rows, 512:1024], ALU.mult, ALU.mult, [xtok, ks, 'gbuf'], [kh1])
            pb, kpb = pbpool.get()
            for k in range(8):
                src = (h0 if k < 4 else h1)[0:rows, (k % 4) * 128:(k % 4) * 128 + 128]
                tr(pb[:, k * 128:k * 128 + rows], src, identb[0:rows, 0:rows], [kh0 if k < 4 else kh1, 'identb'], [kpb])
            cp('act', aT[:, :, col0:col0 + rows], pb[:].rearrange("p (k t) -> p k t", k=8)[:, :, 0:rows], [kpb], [dtok])

        def transpose_into(rows, src_bf, srck, ncb, dst3, cb0, col0, dtoks):
            pb, kpb = pbpool.get()
            for j in range(ncb):
                tr(pb[:, j * 128:j * 128 + rows], src_bf[:, j * 128:(j + 1) * 128], identb[0:rows, 0:rows], [srck, 'identb'], [kpb])
            cp('act', dst3[:, cb0:cb0 + ncb, col0:col0 + rows],
               pb[:, 0:ncb * 128].rearrange("p (k t) -> p k t", k=ncb)[:, :, 0:rows], [kpb], dtoks)

        def load_w(wdram, col0, ncols, kchunks=8):
            w, kw = wpool.get()
            ld('pool', w[:, 0:kchunks, 0:ncols], wdram[:, col0:col0 + ncols].rearrange("(k p) n -> p k n", p=128), [kw])
            return w, kw

        def lin_tok(rows, A3, acol0, atoks, w, kw, ncols, kchunks=8):
            ps, kps = pspool.get()
            for k in range(kchunks):
                mm(ps[0:rows, 0:ncols], A3[:, k, acol0:acol0 + rows], w[:, k, 0:ncols], k == 0, k == kchunks - 1,
                   list(atoks) + [kw], [kps])
            return ps, kps

        def lin_feat(ntok, A3, acol0, atoks, w, kw, wc0, kchunks=8):
            ps, kps = pspool.get()
            for k in range(kchunks):
                mm(ps[:, 0:ntok], w[:, k, wc0:wc0 + 128], A3[:, k, acol0:acol0 + ntok], k == 0, k == kchunks - 1,
                   list(atoks) + [kw], [kps])
            return ps, kps

        ropeC = sb([128, 64])
        ropeS = sb([128, 64])

        def rope_tables(rows, pos, postoks):
            ang, ka = fpool.get()
            ts('dve', ang[0:rows, 0:32], invB[0:rows, :], pos, ALU.mult, reads=['invB'] + list(postoks), writes=[ka])
            cc, kcc = ropeC, 'ropeC'
            ss, kss = ropeS, 'ropeS'
            range_reduce_sin(cc[0:rows, 0:32], ang[0:rows, 0:32], math.pi / 2, rows, 32, [ka], [kcc])
            range_reduce_sin(ss[0:rows, 32:64], ang[0:rows, 0:32], 0.0, rows, 32, [ka], [kss])
            cp('dve', cc[0:rows, 32:64], cc[0:rows, 0:32], [kcc], [kcc])
            ts('dve', ss[0:rows, 0:32], ss[0:rows, 32:64], -1.0, ALU.mult, reads=[kss], writes=[kss])
            return cc, kcc, ss, kss

        def rope_apply(rows, ps, kps, cc, kcc, ss, kss, out32, ko, scale=None):
            x3 = ps[0:rows, :].rearrange("p (h d) -> p h d", h=8)
            o3 = out32[0:rows, :].rearrange("p (h d) -> p h d", h=8)
            t, kt = fpool.get()
            t3 = t[0:rows, :].rearrange("p (h d) -> p h d", h=8)
            tt('dve', o3, x3, cc[0:rows, 0:64].unsqueeze(1).broadcast_to([rows, 8, 64]), ALU.mult, [kps, kcc], [ko])
            tt('dve', t3[:, :, 0:32], x3[:, :, 32:64], ss[0:rows, 0:32].unsqueeze(1).broadcast_to([rows, 8, 32]), ALU.mult, [kps, kss], [kt])
            tt('dve', t3[:, :, 32:64], x3[:, :, 0:32], ss[0:rows, 32:64].unsqueeze(1).broadcast_to([rows, 8, 32]), ALU.mult, [kps, kss, kt], [kt])
            tt('dve', out32[0:rows, :], out32[0:rows, :], t[0:rows, :], ALU.add, [ko, kt], [ko])
            if scale is not None:
                ts('dve', out32[0:rows, :], out32[0:rows, :], scale, ALU.mult, reads=[ko], writes=[ko])

        def ffn_block(layer, ntile, rows, Rtiles, gkey):
            load_gain(gkey)
            for n in range(ntile):
                rmsnorm_T(rows, Rtiles[n][0], Rtiles[n][1], n * 128, aTk(n))
            for j in range(6):
                c0 = j * 512
                nc_ = 512 if j < 5 else 256
                nb = nc_ // 128
                wgb, kwg = load_w(wg[layer], c0, nc_)
                wub, kwu = load_w(wu[layer], c0, nc_)
                wdbuf, kwd = wpool.get()
                wd3 = wdbuf[:].rearrange("p k n -> p (k n)")[:, 0:nb * 1024].rearrange("p (k n) -> p k n", k=nb)
                ld('pool', wd3, wd[layer][c0:c0 + nb * 128, :].rearrange("(k p) n -> p k n", p=128), [kwd])
                for n in range(ntile):
                    pg, kpg = lin_tok(rows, aT, n * 128, [aTk(n)], wgb, kwg, nc_)
                    pu, kpu = lin_tok(rows, aT, n * 128, [aTk(n)], wub, kwu, nc_)
                    sg, ksg = fpool.get()
                    act(sg[0:rows, 0:nc_], pg[0:rows, 0:nc_], AF.Silu, [kpg], [ksg])
                    hb, khb = hpool.get()
                    tt('dve', hb[0:rows, 0:nc_], sg[0:rows, 0:nc_], pu[0:rows, 0:nc_], ALU.mult, [ksg, kpu], [khb])
                    hTt, khT = hpool.get()
                    hT3 = hTt[:].rearrange("p (k t) -> p k t", k=4)
                    transpose_into(rows, hb[0:rows, :], khb, nb, hT3, 0, 0, [khT])
                    for c in range(2):
                        pd, kpd = pspool.get()
                        for k in range(nb):
                            mm(pd[0:rows, :], hT3[:, k, 0:rows], wd3[:, k, c * 512:(c + 1) * 512], k == 0, k == nb - 1, [khT, kwd], [kpd])
                        tt('dve', Rtiles[n][0][:, c * 512:(c + 1) * 512], Rtiles[n][0][:, c * 512:(c + 1) * 512], pd[0:rows, :], ALU.add,
                           [Rtiles[n][1], kpd], [Rtiles[n][1]])

        def out_proj(wdram, ntile, rows, Rtiles):
            for c in range(2):
                w, kw = load_w(wdram, c * 512, 512)
                for n in range(ntile):
                    ps, kps = lin_tok(rows, aT, n * 128, [aTk(n)], w, kw, 512)
                    tt('dve', Rtiles[n][0][:, c * 512:(c + 1) * 512], Rtiles[n][0][:, c * 512:(c + 1) * 512], ps[0:rows, :], ALU.add,
                       [Rtiles[n][1], kps], [Rtiles[n][1]])

        def final_norm_out(rows, xtile, xtok, dst):
            s, ks = rstd_col(rows, xtile, xtok, D, 1e-6)
            stt(xtile, xtile, s[0:rows, 0:1], gbuf[0:rows, :], ALU.mult, ALU.mult, [xtok, ks, 'gbuf'], [xtok])
            st_(dst, xtile, [xtok])

        def s5_chunk(rows, uT_fn, uTtoks, last, final_cb=None):
            for q in range(4):
                bu = []
                for (Btab, BT) in ((BB, BBT), (BBs, BBsT)):
                    for half in range(2):
                        ps, kps = pspool.get()
                        mm(ps[0:rows, :], uT_fn(q), Btab[:, q, 4 * half:4 * half + 4, :].rearrange("p g s -> p (g s)"), True, True,
                           list(uTtoks) + list(BT), [kps])
                        bu.append((ps, kps))
                X = []
                for half in range(2):
                    g0 = 8 * q + 4 * half
                    x1, kx1 = xpool.get()
                    x2, kx2 = xpool.get()
                    tt('dve', x1[0:rows, :], bu[half][0][0:rows, :], WA[0:rows, g0:g0 + 4, :].rearrange("p g s -> p (g s)"), ALU.mult,
                       [bu[half][1]] + WT, [kx1])
                    tt('dve', x2[0:rows, :], bu[2 + half][0][0:rows, :], WB[0:rows, g0:g0 + 4, :].rearrange("p g s -> p (g s)"), ALU.mult,
                       [bu[2 + half][1]] + WT, [kx2])
                    X.append((x1, kx1, x2, kx2))
                for gl in range(8):
                    g = 8 * q + gl
                    x1, kx1, x2, kx2 = X[gl // 4]
                    cs_ = (gl % 4) * 128
                    pg, kpg = pspool.get()
                    ncol = rows + 1
                    mm(pg[:, 0:ncol], x1[0:rows, cs_:cs_ + 128], trie[0:rows, 128 - rows:129], True, False, [kx1, 'trie'], [kpg])
                    mm(pg[:, 0:ncol], x2[0:rows, cs_:cs_ + 128], trie[0:rows, 128 - rows:129], False, True, [kx2, 'trie'], [kpg])
                    z1, kz1 = zpool.get()
                    z2, kz2 = zpool.get()
                    vk = [('VR2', g - g % 4), ('VI2', g - g % 4)]
                    stt(z1[:, 0:rows], pg[:, 0:rows], H0[:, g:g + 1], VR2[:, g, 0:rows], ALU.add, ALU.mult, [kpg, 'H0'] + vk, [kz1])
                    stt(z2[:, 0:rows], pg[:, 0:rows], H0[:, g:g + 1], VI2[:, g, 0:rows], ALU.add, ALU.mult, [kpg, 'H0'] + vk, [kz2])
                    if rows == 128:
                        stt(Zc[:, 0, g:g + 1], pg[:, 128:129], H0[:, g:g + 1], Vc[:, 2, g:g + 1], ALU.add, ALU.mult, [kpg, 'H0'] + VcT, [('Zc', 0, g)])
                        stt(Zc[:, 1, g:g + 1], pg[:, 128:129], H0[:, g:g + 1], Vc[:, 3, g:g + 1], ALU.add, ALU.mult, [kpg, 'H0'] + VcT, [('Zc', 1, g)])
                        if last:
                            stt(Zc[:, 2, g:g + 1], pg[:, 127:128], H0[:, g:g + 1], Vc[:, 0, g:g + 1], ALU.add, ALU.mult, [kpg, 'H0'] + VcT, [('Zc', 2, g)])
                            stt(Zc[:, 3, g:g + 1], pg[:, 127:128], H0[:, g:g + 1], Vc[:, 1, g:g + 1], ALU.add, ALU.mult, [kpg, 'H0'] + VcT, [('Zc', 3, g)])
                    else:
                        tt('dve', Hn[:, g:g + 1], pg[:, 0:1], H0[:, g:g + 1], ALU.add, [kpg, 'H0'], [('Hn', g)])
                    mm(psO[0][0:rows, 16 * g:16 * g + 16], z1[:, 0:rows], CA[:, g, :], True, False, [kz1] + CT, ['psO0'])
                    mm(psO[0][0:rows, 16 * g:16 * g + 16], z2[:, 0:rows], CB[:, g, :], False, True, [kz2] + CT, ['psO0'])
            if rows == 128:
                zr = [('Zc', 0, g) for g in range(32)] + [('Zc', 1, g) for g in range(32)]
                mm(psL[:, 0:32], identf[:], Zc[:, 0, :], True, False, zr + ['identf'], ['psL'])
                mm(psL[:, 0:32], sel2[:], Zc[:, 1, :], False, True, zr + ['sel2'], ['psL'])
                if last:
                    zr2 = [('Zc', 2, g) for g in range(32)] + [('Zc', 3, g) for g in range(32)]
                    mm(psL[:, 32:64], identf[:], Zc[:, 2, :], True, False, zr2 + ['identf'], ['psL'])
                    mm(psL[:, 32:64], sel2[:], Zc[:, 3, :], False, True, zr2 + ['sel2'], ['psL'])
                    final_cb()
                cp('dve', H0[:], psL[:, 0:32], ['psL'], ['H0'])

        def s5_post(rows, u_ps, ku, dst_tiles_fn):
            y32, ky = fpool.get()
            tt('dve', y32[0:rows, :], u_ps[0:rows, :], dB[0:rows, :], ALU.mult, [ku, 'dB'], [ky])
            tt('dve', y32[0:rows, :], y32[0:rows, :], psO[0][0:rows, :], ALU.add, [ky, 'psO0'], [ky])
            return y32, ky

        def glu_and_store(rows, y32, ky, col0, dtoks):
            z32, kz = fpool.get()
            gelu_tanh(z32[0:rows, :], y32[0:rows, :], rows, 512, [ky], [kz])
            zb, kzb = hpool.get()
            cp('act', zb[0:rows, :], z32[0:rows, :], [kz], [kzb])
            zT, kzT = hpool.get()
            zT3 = zT[:].rearrange("p (k t) -> p k t", k=4)
            transpose_into(rows, zb[0:rows, :], kzb, 4, zT3, 0, 0, [kzT])
            ps, kps = lin_tok(rows, zT3, 0, [kzT], wglu_sb, 'wglu', 512, kchunks=4)
            gg, kgg = fpool.get()
            tt('dve', gg[0:rows, :], ps[0:rows, :], bgB[0:rows, :], ALU.add, [kps, 'bgB'], [kgg])
            act(gg[0:rows, :], gg[0:rows, :], AF.Sigmoid, [kgg], [kgg])
            ob, kob = hpool.get()
            tt('dve', ob[0:rows, :], gg[0:rows, :], z32[0:rows, :], ALU.mult, [kgg, kz], [kob])
            transpose_into(rows, ob[0:rows, :], kob, 4, aT, 0, col0, dtoks)

        def subln_store(rows, o1, ko1, hp, col0, dtoks):
            s, ks = rstd_col(rows, o1, ko1, 128, 1e-5)
            ab, kab = hpool.get()
            stt(ab[0:rows, 0:128], o1, s[0:rows, 0:1], sublnB[0:rows, :], ALU.mult, ALU.mult, [ko1, ks, 'sublnB'], [kab])
            transpose_into(rows, ab[0:rows, :], kab, 1, aT, 4 + hp, col0, dtoks)

        wglu_sb = sb([128, 4, 512], BF16)
        ld('pool', wglu_sb[:], w_glu.rearrange("(k p) n -> p k n", p=128), ['wglu'])
        cdbuf = [sb([128, 4, 512]), sb([128, 4, 515]), sb([128, 4, 527])]
        cwc = sb([128, 4, 4])
        for j in range(4):
            cols_from_rows(cwc[:, j, :], conv_w[j, :].rearrange("(b p) -> b p", p=128), 4, [], ('cw', j))
        cwT = [('cw', j) for j in range(4)]
        cbc = sb([128, 4]); cols_from_rows(cbc[:], conv_b.rearrange("(b p) -> b p", p=128), 4, [], 'cbc')
        bac = sb([128, 4]); cols_from_rows(bac[:], lru_ba.rearrange("(b p) -> b p", p=128), 4, [], 'bac')
        bxc = sb([128, 4]); cols_from_rows(bxc[:], lru_bx.rearrange("(b p) -> b p", p=128), 4, [], 'bxc')
        lamc = sb([128, 4]); cols_from_rows(lamc[:], lru_lam.rearrange("(b p) -> b p", p=128), 4, [], 'lamc')
        pscc = sb([128, 4]); cols_from_rows(pscc[:], pool_scale.rearrange("(b p) -> b p", p=128), 4, [], 'pscc')
        act(lamc[:], lamc[:], AF.Exp, ['lamc'], ['lamc'], scale=-1.0)
        act(lamc[:], lamc[:], AF.Ln, ['lamc'], ['lamc'], bias=1.0)
        ts('dve', lamc[:], lamc[:], -8.0, ALU.mult, reads=['lamc'], writes=['lamc'])
        WAb = sb([128, 4, 128], BF16)
        WXb = sb([128, 4, 128], BF16)
        P.op('dve', lambda e: e.memset(WAb[:], 0.0), [], ['WAb'])
        P.op('dve', lambda e: e.memset(WXb[:], 0.0), [], ['WXb'])
        for blk in range(4):
            for h in range(2):
                ld('pool', WAb[64 * h:64 * h + 64, blk, 64 * h:64 * h + 64], lru_wa[2 * blk + h, :, :], ['WAb'], reads=['WAb'])
                ld('pool', WXb[64 * h:64 * h + 64, blk, 64 * h:64 * h + 64], lru_wx[2 * blk + h, :, :], ['WXb'], reads=['WXb'])
        PWb = sb([128, 4, 128], BF16)
        ld('pool', PWb[:], pool_w.rearrange("g c d -> c g d"), ['PWb'])

        def lru_gates(ntok, xc, kxc, blk):
            xcb, kxcb = hpool.get()
            cp('act', xcb[:, 0:ntok], xc, [kxc], [kxcb])
            pr, kpr = pspool.get()
            mm(pr[:, 0:ntok], WAb[:, blk, :], xcb[:, 0:ntok], True, True, ['WAb', kxcb], [kpr])
            pi_, kpi = pspool.get()
            mm(pi_[:, 0:ntok], WXb[:, blk, :], xcb[:, 0:ntok], True, True, ['WXb', kxcb], [kpi])
            rr, krr = fpool.get()
            ii, kii = fpool.get()
            act(rr[:, 0:ntok], pr[:, 0:ntok], AF.Sigmoid, [kpr, 'bac'], [krr], bias=bac[:, blk:blk + 1])
            act(ii[:, 0:ntok], pi_[:, 0:ntok], AF.Sigmoid, [kpi, 'bxc'], [kii], bias=bxc[:, blk:blk + 1])
            aa, kaa = fpool.get()
            act(aa[:, 0:ntok], rr[:, 0:ntok], AF.Exp, [krr, 'lamc'], [kaa], scale=lamc[:, blk:blk + 1])
            bb, kbb = fpool.get()
            tt('dve', bb[:, 0:ntok], aa[:, 0:ntok], aa[:, 0:ntok], ALU.mult, [kaa], [kbb])
            ts('dve', bb[:, 0:ntok], bb[:, 0:ntok], -1.0, ALU.mult, 1.0, ALU.add, reads=[kbb], writes=[kbb])
            ts('dve', bb[:, 0:ntok], bb[:, 0:ntok], 0.0, ALU.max, reads=[kbb], writes=[kbb])
            act(bb[:, 0:ntok], bb[:, 0:ntok], AF.Sqrt, [kbb], [kbb])
            tt('dve', ii[:, 0:ntok], ii[:, 0:ntok], xc, ALU.mult, [kii, kxc], [kii])
            tt('dve', bb[:, 0:ntok], bb[:, 0:ntok], ii[:, 0:ntok], ALU.mult, [kbb, kii], [kbb])
            return aa, kaa, bb, kbb

        uT = sb([128, 4, 512], BF16)
        xlh = sb([128, 4, 3])
        xph = sb([128, 4, 15])
        hprev = sb([128, 4])
        bigA = sb([128, 528])
        bigB = sb([128, 528])
        P.op('dve', lambda e: e.memset(xlh[:], 0.0), [], ['xlh'])
        P.op('dve', lambda e: e.memset(xph[:], 0.0), [], ['xph'])
        P.op('dve', lambda e: e.memset(hprev[:], 0.0), [], ['hprev'])

        for st in range(n_st):
            Rt = [(R[:, n, :], ('R', n)) for n in range(4)]
            t0 = st * 512
            for n in range(4):
                ld('sp', R[:, n, :], xp[t0 + n * 128:t0 + n * 128 + 128, :], [('R', n)])
            load_gain('mix0')
            for n in range(4):
                rmsnorm_T(128, R[:, n, :], ('R', n), n * 128, aTk(n))
            wq, kwq = load_w(w_in_ab[:, :], 512, 512)
            wk, kwk = load_w(w_in_ab[:, :], 1024, 512)
            for n in range(4):
                poscol, kpos = spool.get()
                ts('dve', poscol[:, 0:1], partc[:], float(t0 + n * 128), ALU.add, reads=['partc'], writes=[kpos])
                cc, kcc, ss, kss = rope_tables(128, poscol[:, 0:1], [kpos])
                ps, kps = lin_tok(128, aT, n * 128, [aTk(n)], wq, kwq, 512)
                q32, kq32 = fpool.get()
                rope_apply(128, ps, kps, cc, kcc, ss, kss, q32, kq32, scale=0.125)
                qb, kqb = hpool.get()
                cp('act', qb[:], q32[:], [kq32], [kqb])
                transpose_into(128, qb, kqb, 4, QT, 0, n * 128, [('QT', n)])
                ps, kps = lin_tok(128, aT, n * 128, [aTk(n)], wk, kwk, 512)
                k32, kk32 = fpool.get()
                rope_apply(128, ps, kps, cc, kcc, ss, kss, k32, kk32)
                st_(k_p[t0 + n * 128:t0 + n * 128 + 128, :], k32[:], [kk32])
                kb, kkb = hpool.get()
                cp('act', kb[:], k32[:], [kk32], [kkb])
                ktt, kktt = hpool.get()
                kt3 = ktt[:].rearrange("p (k t) -> p k t", k=4)
                transpose_into(128, kb, kkb, 4, kt3, 0, 0, [kktt])
                st_(kt_scr[:, :, t0 + n * 128:t0 + n * 128 + 128].rearrange("h p t -> p h t"), kt3, [kktt], [('ktscr', st * 4 + n)])
            wv, kwv = load_w(w_in_ab[:, :], 1536, 512)
            for n in range(4):
                ps, kps = lin_tok(128, aT, n * 128, [aTk(n)], wv, kwv, 512)
                v32, kv32 = fpool.get()
                cp('act', v32[:], ps[:], [kps], [kv32])
                st_(v_p[t0 + n * 128:t0 + n * 128 + 128, :], v32[:], [kv32], [('vout', st * 4 + n)])
            wu_, kwu_ = load_w(w_in_ab[:, :], 0, 512)
            for q in range(4):
                ps, kps = lin_feat(512, aT, 0, aT_all, wu_, kwu_, q * 128)
                cp('act', uT[:, q, :], ps[:], [kps], [('uT', q)])
            for n in range(4):
                last = (st == n_st - 1 and n == 3)

                def fin():
                    hf, khf = fpool.get()
                    cp('dve', hf[:, 0:32], psL[:, 32:64], ['psL'], [khf])
                    ps, kps = pspool.get()
                    tr(ps[0:32, 0:128], hf[:, 0:32], identf[:], [khf, 'identf'], [kps])
                    o, ko = fpool.get()
                    cp('dve', o[0:32, 0:128], ps[0:32, 0:128], [kps], [ko])
                    st_(re_p[:, :], o[0:32, 0:64], [ko])
                    st_(im_p[:, :], o[0:32, 64:128], [ko])
                pu_, kpu_ = lin_tok(128, aT, n * 128, [aTk(n)], wu_, kwu_, 512)
                ut, kut = fpool.get()
                cp('act', ut[:], pu_[:], [kpu_], [kut])
                s5_chunk(128, lambda q, n=n: uT[:, q, n * 128:n * 128 + 128], [('uT', q) for q in range(4)], last, fin)
                y32, ky = s5_post(128, ut, kut, None)
                glu_and_store(128, y32, ky, n * 128, [aTk(n)])
            nkt = 4 * (st + 1)
            for hp in range(4):
                for kg in range(0, nkt, 4):
                    kts, kkts = kpool.get()
                    ld('sp', kts[:, 0:512], kt_scr[hp, :, kg * 128:kg * 128 + 512], [kkts], reads=[('ktscr', kg + i) for i in range(4)])
                    for kt in range(kg, kg + 4):
                        vt, kvt = hpool.get()
                        ld('pool', vt[:, 0:128], v_p[kt * 128:kt * 128 + 128, hp * 128:hp * 128 + 128], [kvt], reads=[('vout', kt)])
                        for j in range(2):
                            ps, kps = pspool.get()
                            mm(ps[:], kts[64 * j:64 * j + 64, (kt - kg) * 128:(kt - kg) * 128 + 128], QT[64 * j:64 * j + 64, hp, :], True, True,
                               [kkts] + [('QT', n) for n in range(4)], [kps])
                            pT, kpT = hpool.get()
                            act(pT[:], ps[:], AF.Exp, [kps], [kpT])
                            di = kt - 4 * st
                            if di >= 0:
                                tt('pool', pT[:], pT[:], masks[:, di, :], ALU.mult, [kpT, ('mask', di)], [kpT])
                            mm(psO[j][:], vt[:, 0:128], pT[:], kt == 0, kt == nkt - 1, [kvt, kpT], ['psO%d' % j])
                            mm(psL[0:2, 0:512], E2[:, j, :], pT[:], kt == 0 and j == 0, kt == nkt - 1 and j == 1, [kpT, 'E2'], ['psL'])
                lsb, klsb = fpool.get()
                cp('dve', lsb[0:2, :], psL[0:2, 0:512], ['psL'], [klsb])
                osb = []
                for j in range(2):
                    o, ko = fpool.get()
                    cp('act', o[:], psO[j][:], ['psO%d' % j], [ko])
                    osb.append((o, ko))
                for n in range(4):
                    pl, kpl = pspool.get()
                    tr(pl[:, 0:2], lsb[0:2, n * 128:n * 128 + 128], identf[0:2, 0:2], [klsb, 'identf'], [kpl])
                    rl, krl = spool.get()
                    recip(rl[:, 0:2], pl[:, 0:2], [kpl], [krl])
                    tt('dve', rl[:, 1:2], rl[:, 1:2], lamt[:, 1:2], ALU.mult, [krl, 'lamt'], [krl])
                    po = []
                    for j in range(2):
                        pt_, kpt_ = pspool.get()
                        tr(pt_[:, 0:128], osb[j][0][:, n * 128:n * 128 + 128], identf[:], [osb[j][1], 'identf'], [kpt_])
                        po.append((pt_, kpt_))
                    o1, ko1 = fpool.get()
                    ts('dve', o1[:, 0:128], po[0][0][:, 0:128], rl[:, 0:1], ALU.mult, reads=[po[0][1], krl], writes=[ko1])
                    stt(o1[:, 0:128], po[1][0][:, 0:128], rl[:, 1:2], o1[:, 0:128], ALU.mult, ALU.add, [po[1][1], krl, ko1], [ko1])
                    subln_store(128, o1[:, 0:128], ko1, hp, n * 128, [aTk(n)])
            def dump_aT(i):
                if debug and st == 0:
                    P.dma('sp', lambda e: e.dma_start(out=dbgA[i].rearrange("k p t -> p k t"), in_=aT[:]), reads=aT_all)

            def dump_R(i):
                if debug and st == 0:
                    for n in range(4):
                        st_(dbgR[i, n * 128:n * 128 + 128, :], R[:, n, :], [('R', n)])
            dump_aT(0)
            out_proj(w_out_ab[:, :], 4, 128, Rt)
            dump_R(0)
            ffn_block(0, 4, 128, Rt, 'ffn0')
            dump_R(1)
            load_gain('mix1')
            for n in range(4):
                rmsnorm_T(128, R[:, n, :], ('R', n), n * 128, aTk(n))
            ext = {}
            for part, halo, hbuf, hk in ((0, 0, None, None), (1, 3, xlh, 'xlh'), (2, 15, xph, 'xph')):
                w, kw = load_w(w_in_cd[:, :], part * 512, 512)
                for blk in range(4):
                    ps, kps = lin_feat(512, aT, 0, aT_all, w, kw, blk * 128)
                    dstt = cdbuf[part][:, blk, halo:halo + 512]
                    cp('act', dstt, ps[:], [kps], [('cd', part, blk)])
                    if halo:
                        cp('dve', cdbuf[part][:, blk, 0:halo], hbuf[:, blk, :], [hk], [('cdh', part, blk)])
            for blk in range(4):
                xl = cdbuf[1][:, blk, :]
                xlk = [('cd', 1, blk), ('cdh', 1, blk)]
                xc, kxc = fpool.get()
                ts('dve', xc[:], xl[:, 0:512], cwc[:, 0, blk:blk + 1], ALU.mult, cbc[:, blk:blk + 1], ALU.add, reads=xlk + cwT + ['cbc'], writes=[kxc])
                for j in range(1, 4):
                    stt(xc[:], xl[:, j:j + 512], cwc[:, j, blk:blk + 1], xc[:], ALU.mult, ALU.add, xlk + cwT + [kxc], [kxc])
                aa, kaa, bb, kbb = lru_gates(512, xc[:], kxc, blk)
                hs, khs = fpool.get()
                P.op('dve', lambda e, hs=hs, aa=aa, bb=bb, blk=blk: e.tensor_tensor_scan(out=hs[:], data0=aa[:], data1=bb[:], initial=hprev[:, blk:blk + 1],
                                                                                  op0=ALU.mult, op1=ALU.add), [kaa, kbb, 'hprev'], [khs])
                cp('dve', hprev[:, blk:blk + 1], hs[:, 511:512], [khs], ['hprev'])
                gl_, kgl = fpool.get()
                gelu_tanh(gl_[:], cdbuf[0][:, blk, :], 128, 512, [('cd', 0, blk)], [kgl])
                tt('dve', aT[:, blk, :], gl_[:], hs[:], ALU.mult, [kgl, khs], aT_all)
                wdw = 2 ** (blk + 1)
                xpv = cdbuf[2][:, blk, :]
                xpk = [('cd', 2, blk), ('cdh', 2, blk)]
                src_ap, src_k, width = xpv, xpk, 527
                m = 1
                flip = 0
                while m < wdw:
                    nw = width - m
                    big = bigA if flip == 0 else bigB
                    kbig = 'bigA' if flip == 0 else 'bigB'
                    tt('dve', big[:, 0:nw], src_ap[:, m:m + nw], src_ap[:, 0:nw], ALU.add, list(src_k), [kbig])
                    src_ap, src_k, width = big, [kbig], nw
                    m *= 2
                    flip ^= 1
                o0 = 15 - (wdw - 1)
                pl_, kpl_ = fpool.get()
                if st == 0:
                    ts('dve', pl_[:], iota[:, 0:512], 1.0, ALU.add, float(wdw), ALU.min, reads=['iota'], writes=[kpl_])
                    recip(pl_[:], pl_[:], [kpl_], [kpl_])
                    tt('dve', pl_[:], pl_[:], src_ap[:, o0:o0 + 512], ALU.mult, [kpl_] + list(src_k), [kpl_])
                else:
                    ts('dve', pl_[:], src_ap[:, o0:o0 + 512], 1.0 / wdw, ALU.mult, reads=list(src_k), writes=[kpl_])
                plb, kplb = hpool.get()
                tt('dve', plb[:], pl_[:], xpv[:, 15:527], ALU.subtract, [kpl_] + xpk, [kplb])
                pp, kpp = pspool.get()
                mm(pp[:], PWb[:, blk, :], plb[:], True, True, ['PWb', kplb], [kpp])
                ts('dve', aT[:, 4 + blk, :], pp[:], pscc[:, blk:blk + 1], ALU.mult, reads=[kpp, 'pscc'], writes=aT_all)
                cp('dve', xlh[:, blk, :], xl[:, 512:515], xlk, ['xlh'])
                cp('dve', xph[:, blk, :], xpv[:, 512:527], xpk, ['xph'])
                if st == n_st - 1:
                    rows_out(conv_p[:, blk * 128:blk * 128 + 128], xl[:, 512:515], 3, xlk)
                    rows_out(pool_p[:, blk * 128:blk * 128 + 128], xpv[:, 512:527], 15, xpk)
            if st == n_st - 1:
                rows_out(lru_p.rearrange("(b p) -> b p", p=128), hprev[:], 4, ['hprev'])
            dump_aT(1)
            out_proj(w_out_cd[:, :], 4, 128, Rt)
            dump_R(2)
            ffn_block(1, 4, 128, Rt, 'ffn1')
            dump_R(3)
            load_gain('fin')
            for n in range(4):
                final_norm_out(128, R[:, n, :], ('R', n), y_p[t0 + n * 128:t0 + n * 128 + 128, :])

        if do_sample:
            Rs = R[0:4, 0, :]
            RsK = ('R', 0)
            Rts = [(Rs, RsK)]
            ld('sp', Rs, xs[:, :], [RsK])
            load_gain('mix0')
            rmsnorm_T(4, Rs, RsK, 0, aTk(0))
            cc, kcc, ss, kss = rope_tables(4, float(PAST), [])
            q32s = cdbuf[0][0:4, 0, :]
            k32s = cdbuf[0][0:4, 1, :]
            v32s = cdbuf[0][0:4, 2, :]
            u32s = cdbuf[0][0:4, 3, :]
            uTs = sb([128, 4, 4], BF16)
            w, kw = load_w(w_in_ab[:, :], 512, 512)
            ps, kps = lin_tok(4, aT, 0, [aTk(0)], w, kw, 512)
            rope_apply(4, ps, kps, cc, kcc, ss, kss, q32s, 'q32s', scale=0.125)
            w, kw = load_w(w_in_ab[:, :], 1024, 512)
            ps, kps = lin_tok(4, aT, 0, [aTk(0)], w, kw, 512)
            rope_apply(4, ps, kps, cc, kcc, ss, kss, k32s, 'k32s')
            st_(k_s[:, :], k32s, ['k32s'])
            w, kw = load_w(w_in_ab[:, :], 1536, 512)
            ps, kps = lin_tok(4, aT, 0, [aTk(0)], w, kw, 512)
            cp('act', v32s, ps[0:4, :], [kps], ['v32s'])
            st_(v_s[:, :], v32s, ['v32s'])
            w, kw = load_w(w_in_ab[:, :], 0, 512)
            ps, kps = lin_tok(4, aT, 0, [aTk(0)], w, kw, 512)
            cp('act', u32s, ps[0:4, :], [kps], ['u32s'])
            for q in range(4):
                ps, kps = lin_feat(4, aT, 0, [aTk(0)], w, kw, q * 128)
                cp('act', uTs[:, q, :], ps[:, 0:4], [kps], [('uTs', q)])
            for r in range(4):
                hin, khin = fpool.get()
                ld('sp', hin[0:32, 0:64], st_re[r, :, :], [khin])
                ld('sp', hin[0:32, 64:128], st_im[r, :, :], [khin], reads=[khin])
                ps, kps = pspool.get()
                tr(ps[:, 0:32], hin[0:32, 0:128], identf[0:32, 0:32], [khin, 'identf'], [kps])
                z1, kz1 = fpool.get()
                z2, kz2 = fpool.get()
                tt('dve', z1[:, 0:32], ps[:, 0:32], lb[:, 0, :], ALU.mult, [kps, 'lb0'], [kz1])
                tt('dve', z2[:, 0:32], ps[:, 0:32], lb[:, 1, :], ALU.mult, [kps, 'lb1'], [kz2])
                mm(psL[:, 0:32], identf[:], z1[:, 0:32], True, False, [kz1, 'identf'], ['psL'])
                mm(psL[:, 0:32], sel2[:], z2[:, 0:32], False, True, [kz2, 'sel2'], ['psL'])
                cp('dve', H0[:], psL[:, 0:32], ['psL'], ['H0'])
                s5_chunk(1, lambda q, r=r: uTs[:, q, r:r + 1], [('uTs', q) for q in range(4)], False)
                ps, kps = pspool.get()
                tr(ps[0:32, 0:128], Hn[:], identf[:], [('Hn', g) for g in range(32)] + ['identf'], [kps])
                o, ko = fpool.get()
                cp('dve', o[0:32, 0:128], ps[0:32, 0:128], [kps], [ko])
                st_(re_s[r, :, :], o[0:32, 0:64], [ko])
                st_(im_s[r, :, :], o[0:32, 64:128], [ko])
                yr, kyr = fpool.get()
                cp('dve', yr[0:1, :], psO[0][0:1, :], ['psO0'], [kyr])
                mm(psO[1][0:4, :], oh14[0:1, r, :], yr[0:1, :], r == 0, r == 3, [kyr, 'oh14'], ['psO1'])
            y32, ky = fpool.get()
            tt('dve', y32[0:4, :], u32s, dB[0:4, :], ALU.mult, ['u32s', 'dB'], [ky])
            tt('dve', y32[0:4, :], y32[0:4, :], psO[1][0:4, :], ALU.add, [ky, 'psO1'], [ky])
            glu_and_store(4, y32, ky, 0, [aTk(0)])
            it = sb([128, 4], I32)
            ld('sp', it[:], pt[:, :], ['it'])
            itf = sb([128, 4])
            ts('dve', itf[:], it[:], 128.0, ALU.mult, reads=['it'], writes=['itf'])
            ipool = RPool('ix', [sb([128, 1], I32) for _ in range(4)])
            ckrows = ck.rearrange("n (t c) -> (n t) c", c=512)
            cvrows = cv.rearrange("n (t c) -> (n t) c", c=512)
            pself = sb([4, 8])
            t, kt = fpool.get()
            tt('dve', t[0:4, :], q32s, k32s, ALU.mult, ['q32s', 'k32s'], [kt])
            P.op('dve', lambda e: e.tensor_reduce(out=pself[:], in_=t[0:4, :].rearrange("p (h d) -> p h d", h=8), axis=AX.X, op=ALU.add), [kt], ['pself'])
            act(pself[:], pself[:], AF.Exp, ['pself'], ['pself'])
            sg8 = sb([8, 1])
            ts('dve', sg8[:], altc[:], lamt[0:8, 2:3], ALU.mult, 1.0, ALU.add, reads=['altc', 'lamt'], writes=['sg8'])
            qB = cdbuf[1][:, 0, 0:512]
            for r in range(4):
                ps, kps = pspool.get()
                mm(ps[:], oh44[0:4, r, :], q32s, True, True, ['oh44', 'q32s'], [kps])
                cp('act', qB, ps[:], [kps], ['qB'])
                for tk in range(128):
                    Kc, kKc = fpool.get()
                    Vc_, kVc = fpool.get()
                    ix, kix = ipool.get()
                    ts('dve', ix[:], itf[:, r:r + 1], float(tk), ALU.add, reads=['itf'], writes=[kix])
                    P.dma('pool', lambda e, Kc=Kc, ix=ix: e.indirect_dma_start(
                        out=Kc[:], out_offset=None, in_=ckrows[:, :],
                        in_offset=bass.IndirectOffsetOnAxis(ap=ix[:, :], axis=0)),
                        reads=[kix], writes=[kKc])
                    P.dma('pool', lambda e, Vc_=Vc_, ix=ix: e.indirect_dma_start(
                        out=Vc_[:], out_offset=None, in_=cvrows[:, :],
                        in_offset=bass.IndirectOffsetOnAxis(ap=ix[:, :], axis=0)),
                        reads=[kix], writes=[kVc])
                    pr, kpr = fpool.get()
                    tt('dve', pr[:], Kc[:], qB, ALU.mult, [kKc, 'qB'], [kpr])
                    sc, ksc = spool.get()
                    P.op('dve', lambda e, sc=sc, pr=pr: e.tensor_reduce(out=sc[:, 0:8], in_=pr[:].rearrange("p (h d) -> p h d", h=8), axis=AX.X, op=ALU.add), [kpr], [ksc])
                    act(sc[:, 0:8], sc[:, 0:8], AF.Exp, [ksc], [ksc])
                    mm(psO[0][0:8, :], sc[:, 0:8], Vc_[:], tk == 0, False, [ksc, kVc], ['psO0'])
                    mm(psL[0:1, 0:8], ones_f[:, 0:1], sc[:, 0:8], tk == 0, False, [ksc, 'ones_f'], ['psL'])
                pm, kpm = spool.get()
                ts('dve', pm[0:4, 0:8], pself[:], oh44[0:4, r, 0:1], ALU.mult, reads=['pself', 'oh44'], writes=[kpm])
                mm(psO[0][0:8, :], pm[0:4, 0:8], v32s, False, True, [kpm, 'v32s'], ['psO0'])
                mm(psL[0:1, 0:8], oh44[0:4, r, 0:1], pself[:], False, True, ['pself', 'oh44'], ['psL'])
                lrow, klrow = fpool.get()
                cp('dve', lrow[0:1, 0:8], psL[0:1, 0:8], ['psL'], [klrow])
                ps, kps = pspool.get()
                tr(ps[0:8, 0:1], lrow[0:1, 0:8], identf[0:1, 0:1], [klrow, 'identf'], [kps])
                cv8, kcv8 = spool.get()
                recip(cv8[0:8, 0:1], ps[0:8, 0:1], [kps], [kcv8])
                tt('dve', cv8[0:8, 0:1], cv8[0:8, 0:1], sg8[:], ALU.mult, [kcv8, 'sg8'], [kcv8])
                osb, kosb = fpool.get()
                stt(osb[0:8, :], psO[0][0:8, :], cv8[0:8, 0:1], bmask[:], ALU.mult, ALU.mult, ['psO0', kcv8, 'bmask'], [kosb])
                mm(psO[1][0:4, :], oh84[:, r, :], osb[0:8, :], r == 0, r == 3, [kosb, 'oh84'], ['psO1'])
            a4, ka4 = fpool.get()
            cp('dve', a4[0:4, :], psO[1][0:4, :], ['psO1'], [ka4])
            for hp in range(4):
                subln_store(4, a4[0:4, hp * 128:hp * 128 + 128], ka4, hp, 0, [aTk(0)])
            if debug:
                P.dma('sp', lambda e: e.dma_start(out=dbgS.rearrange("k p t -> p k t"), in_=aT[:, :, 0:4]), reads=[aTk(0)])
            out_proj(w_out_ab[:, :], 1, 4, Rts)
            if debug:
                st_(dbgRs[0], Rs, [RsK])
            ffn_block(0, 1, 4, Rts, 'ffn0')
            if debug:
                st_(dbgRs[1], Rs, [RsK])
            load_gain('mix1')
            rmsnorm_T(4, Rs, RsK, 0, aTk(0))
            sg = sb([128, 3, 4, 4])
            for part in range(3):
                w, kw = load_w(w_in_cd[:, :], part * 512, 512)
                for blk in range(4):
                    ps, kps = lin_feat(4, aT, 0, [aTk(0)], w, kw, blk * 128)
                    cp('act', sg[:, part, blk, :], ps[:, 0:4], [kps], [('sg', part, blk)])
            csT = sb([128, 4, 12])
            lrT = sb([128, 4, 4])
            psT = sb([128, 4, 60])
            cso = sb([128, 4, 12])
            pso = sb([128, 4, 60])
            hso = sb([128, 4, 4])
            c2 = st_conv.rearrange("r j c -> (r j) c")
            p2 = st_pool.rearrange("r j c -> (r j) c")
            for blk in range(4):
                bs = slice(blk * 128, blk * 128 + 128)
                cols_from_rows(csT[:, blk, :], c2[:, bs], 12, [], ('csT', blk))
                cols_from_rows(lrT[:, blk, :], st_lru[:, bs], 4, [], ('lrT', blk))
                cols_from_rows(psT[:, blk, :], p2[:, bs], 60, [], ('psT', blk))
                cs3 = csT[:, blk, :].rearrange("p (r j) -> p r j", r=4)
                xl = sg[:, 1, blk, :]
                xc, kxc = fpool.get()
                ts('dve', xc[:, 0:4], cs3[:, :, 0], cwc[:, 0, blk:blk + 1], ALU.mult, cbc[:, blk:blk + 1], ALU.add, reads=[('csT', blk)] + cwT + ['cbc'], writes=[kxc])
                for j in (1, 2):
                    stt(xc[:, 0:4], cs3[:, :, j], cwc[:, j, blk:blk + 1], xc[:, 0:4], ALU.mult, ALU.add, [('csT', blk)] + cwT + [kxc], [kxc])
                stt(xc[:, 0:4], xl, cwc[:, 3, blk:blk + 1], xc[:, 0:4], ALU.mult, ALU.add, [('sg', 1, blk)] + cwT + [kxc], [kxc])
                co3 = cso[:, blk, :].rearrange("p (r j) -> p r j", r=4)
                cp('dve', co3[:, :, 0:2], cs3[:, :, 1:3], [('csT', blk)], [('cso', blk)])
                cp('dve', co3[:, :, 2], xl, [('sg', 1, blk), ('cso', blk)], [('cso', blk)])
                rows_out(conv_s.rearrange("r j c -> (r j) c")[:, bs], cso[:, blk, :], 12, [('cso', blk)])
                aa, kaa, bb, kbb = lru_gates(4, xc[:, 0:4], kxc, blk)
                tt('dve', hso[:, blk, :], aa[:, 0:4], lrT[:, blk, :], ALU.mult, [kaa, ('lrT', blk)], [('hso', blk)])
                tt('dve', hso[:, blk, :], hso[:, blk, :], bb[:, 0:4], ALU.add, [kbb, ('hso', blk)], [('hso', blk)])
                rows_out(lru_s[:, bs], hso[:, blk, :], 4, [('hso', blk)])
                gl_, kgl = fpool.get()
                gelu_tanh(gl_[:, 0:4], sg[:, 0, blk, :], 128, 4, [('sg', 0, blk)], [kgl])
                tt('dve', aT[:, blk, 0:4], gl_[:, 0:4], hso[:, blk, :], ALU.mult, [kgl, ('hso', blk)], [aTk(0)])
                wdw = 2 ** (blk + 1)
                ps3 = psT[:, blk, :].rearrange("p (r j) -> p r j", r=4)
                xpv = sg[:, 2, blk, :]
                wsum, kws = fpool.get()
                P.op('dve', lambda e, wsum=wsum, ps3=ps3, wdw=wdw: e.tensor_reduce(out=wsum[:, 0:4], in_=ps3[:, :, 16 - wdw:15], axis=AX.X, op=ALU.add), [('psT', blk)], [kws])
                tt('dve', wsum[:, 0:4], wsum[:, 0:4], xpv, ALU.add, [kws, ('sg', 2, blk)], [kws])
                plb, kplb = hpool.get()
                stt(plb[:, 0:4], wsum[:, 0:4], 1.0 / wdw, xpv, ALU.mult, ALU.subtract, [kws, ('sg', 2, blk)], [kplb])
                pp, kpp = pspool.get()
                mm(pp[:, 0:4], PWb[:, blk, :], plb[:, 0:4], True, True, ['PWb', kplb], [kpp])
                ts('dve', aT[:, 4 + blk, 0:4], pp[:, 0:4], pscc[:, blk:blk + 1], ALU.mult, reads=[kpp, 'pscc'], writes=[aTk(0)])
                po3 = pso[:, blk, :].rearrange("p (r j) -> p r j", r=4)
                cp('dve', po3[:, :, 0:14], ps3[:, :, 1:15], [('psT', blk)], [('pso', blk)])
                cp('dve', po3[:, :, 14], xpv, [('sg', 2, blk), ('pso', blk)], [('pso', blk)])
                rows_out(pool_s.rearrange("r j c -> (r j) c")[:, bs], pso[:, blk, :], 60, [('pso', blk)])
            out_proj(w_out_cd[:, :], 1, 4, Rts)
            ffn_block(1, 1, 4, Rts, 'ffn1')
            load_gain('fin')
            final_norm_out(4, Rs, RsK, y_s[:, :])

        P.emit()
    return nc, dt_in


_CACHE = {}


def _consts():
    c = {}
    c["c_ident"] = np.eye(128, dtype=np.float32)
    tri = np.zeros((128, 129), np.float32)
    for s in range(128):
        tri[s, s:128] = 1.0
    tri[:, 128] = 1.0
    c["c_trie"] = tri
    sel2 = np.zeros((128, 128), np.float32)
    for m in range(64):
        sel2[m + 64, m] = -1.0
        sel2[m, m + 64] = 1.0
    c["c_sel2"] = sel2
    gm = np.zeros((128, 8), np.float32)
    for p in range(128):
        gm[p, p // 16] = 1.0
    c["c_gmask"] = gm
    c["c_iota"] = np.tile(np.arange(640, dtype=np.float32)[None, :], (128, 1))
    c["c_part"] = np.arange(128, dtype=np.float32)[:, None].copy()
    bm = np.zeros((8, 512), np.float32)
    for h in range(8):
        bm[h, (h // 2) * 128:(h // 2) * 128 + 128] = 1.0
    c["c_bmask"] = bm
    oh84 = np.zeros((8, 4, 4), np.float32)
    oh44 = np.zeros((4, 4, 128), np.float32)
    oh14 = np.zeros((1, 4, 4), np.float32)
    for r in range(4):
        oh84[:, r, r] = 1.0
        oh44[r, r, :] = 1.0
        oh14[0, r, r] = 1.0
    c["c_oh84"], c["c_oh44"], c["c_oh14"] = oh84, oh44, oh14
    c["c_alt"] = (np.arange(8) % 2).astype(np.float32)[:, None].copy()
    return c


def _run(inputs, n_st=NST, do_sample=True, trace=False, debug=False, compact=False):
    key = (n_st, do_sample, debug, compact)
    if key not in _CACHE:
        _CACHE[key] = build(n_st, do_sample, debug, 512 if compact else NPOOL)
    nc, dt_in = _CACHE[key]
    f = lambda a: np.ascontiguousarray(np.asarray(a))
    I = {k: f(v) for k, v in inputs.items()}
    consts = _consts()
    shared = {
        "norm_mix": I["norm_mix"], "norm_ffn": I["norm_ffn"], "norm_final": I["norm_final"],
        "w_in_ab": I["w_in_ab"][0], "w_out_ab": I["w_out_ab"][0],
        "s5_a_re": I["s5_a_re"][0], "s5_a_im": I["s5_a_im"][0], "s5_log_dt": I["s5_log_dt"][0],
        "s5_b_re": I["s5_b_re"][0], "s5_b_im": I["s5_b_im"][0], "s5_c_re": I["s5_c_re"][0], "s5_c_im": I["s5_c_im"][0],
        "s5_d": I["s5_d"][0], "s5_w_glu": I["s5_w_glu"][0], "s5_b_glu": I["s5_b_glu"][0],
        "diff_lq1": I["diff_lq1"][0], "diff_lk1": I["diff_lk1"][0], "diff_lq2": I["diff_lq2"][0], "diff_lk2": I["diff_lk2"][0],
        "diff_subln": I["diff_subln"][0],
        "w_in_cd": I["w_in_cd"][0], "w_out_cd": I["w_out_cd"][0], "conv_w": I["conv_w"][0], "conv_b": I["conv_b"][0],
        "lru_wa": I["lru_wa"][0], "lru_ba": I["lru_ba"][0], "lru_wx": I["lru_wx"][0], "lru_bx": I["lru_bx"][0],
        "lru_lambda": I["lru_lambda"][0], "pool_w": I["pool_w"][0], "pool_scale": I["pool_scale"][0],
        "ffn_w_gate": I["ffn_w_gate"], "ffn_w_up": I["ffn_w_up"], "ffn_w_down": I["ffn_w_down"],
    }
    shared.update(consts)
    if do_sample and not compact:
        shared["ck"] = I["cache_k"][0].reshape(NPOOL, 128 * 512)
        shared["cv"] = I["cache_v"][0].reshape(NPOOL, 128 * 512)
    in_maps = []
    for c in range(8):
        m = dict(shared)
        m["xp"] = I["x_prompt"][c % 2]
        sl = slice(4 * c, 4 * c + 4)
        m["xs"] = I["x_sample"][sl, 0, :]
        m["pt"] = np.ascontiguousarray(I["page_table"][sl].T.astype(np.int32))
        if compact and do_sample:
            pg = I["page_table"][sl].reshape(-1)
            m["ck"] = I["cache_k"][0][pg].reshape(512, 128 * 512)
            m["cv"] = I["cache_v"][0][pg].reshape(512, 128 * 512)
            m["pt"] = np.ascontiguousarray(np.arange(512, dtype=np.int32).reshape(4, 128).T)
        m["st_re"] = I["state_s5_re"][0][sl]
        m["st_im"] = I["state_s5_im"][0][sl]
        m["st_conv"] = I["state_conv"][0][sl]
        m["st_lru"] = I["state_lru"][0][sl]
        m["st_pool"] = I["state_pool"][0][sl]
        m = {k: np.ascontiguousarray(v) for k, v in m.items() if k in dt_in}
        in_maps.append(m)
    res = run_bass_kernel_spmd(nc, in_maps, core_ids=list(range(8)), **({"trace": True} if trace else {}))
    R = res.results
    L = n_st * 512

    def pr(name, shape):
        return np.stack([R[b][name].reshape(shape) for b in range(2)])[None] if True else None
    y_prompt = np.stack([R[b]["y_p"] for b in range(2)])
    y_sample = np.concatenate([R[c]["y_s"] for c in range(8)], 0)[:, None, :]
    k_prompt = np.stack([R[b]["k_p"].reshape(SEQ, 8, 64) for b in range(2)])[None]
    v_prompt = np.stack([R[b]["v_p"].reshape(SEQ, 4, 128) for b in range(2)])[None]
    k_sample = np.concatenate([R[c]["k_s"] for c in range(8)], 0).reshape(1, 32, 1, 8, 64)
    v_sample = np.concatenate([R[c]["v_s"] for c in range(8)], 0).reshape(1, 32, 1, 4, 128)
    re_p = np.stack([R[b]["re_p"] for b in range(2)])[None]
    im_p = np.stack([R[b]["im_p"] for b in range(2)])[None]
    re_s = np.concatenate([R[c]["re_s"] for c in range(8)], 0)[None]
    im_s = np.concatenate([R[c]["im_s"] for c in range(8)], 0)[None]
    conv_p = np.stack([R[b]["conv_p"] for b in range(2)])[None]
    conv_s = np.concatenate([R[c]["conv_s"] for c in range(8)], 0)[None]
    lru_p = np.stack([R[b]["lru_p"] for b in range(2)])[None]
    lru_s = np.concatenate([R[c]["lru_s"] for c in range(8)], 0)[None]
    pool_p = np.stack([R[b]["pool_p"] for b in range(2)])[None]
    pool_s = np.concatenate([R[c]["pool_s"] for c in range(8)], 0)[None]
    outs = (y_prompt, y_sample, k_prompt, v_prompt, k_sample, v_sample, re_p, im_p, re_s, im_s,
            conv_p, conv_s, lru_p, lru_s, pool_p, pool_s)
    outs = tuple(np.ascontiguousarray(o.astype(np.float32)) for o in outs)
    return outs, res


def kernel(**inputs):
    outs, _ = _run(inputs)
    return outs
```

```python
import math
import contextlib
import numpy as np
import concourse.bass as bass
import concourse.mybir as mybir
from concourse.bass_utils import run_bass_kernel_spmd

F32 = mybir.dt.float32
BF16 = mybir.dt.bfloat16
I32 = mybir.dt.int32
AF = mybir.ActivationFunctionType
ALU = mybir.AluOpType
AX = mybir.AxisListType

ENGS = ['pe', 'act', 'dve', 'pool', 'sp']
NDSEM = 94
DQ = {'sp': (0, 47), 'pool': (47, 47), 'act': (0, 0)}

D = 1024
SEQ = 8192
NT = SEQ // 128
NST = SEQ // 512
FFN = 2816
PAST = 16384
NPAGE = 128
NPOOL = 5120
TWO_PI = 2.0 * math.pi
MAGIC = 12582912.0


class Prog:
    def __init__(self, nc):
        self.nc = nc
        self.ops = {e: [] for e in ENGS}
        self.cnt = {e: 0 for e in ENGS}
        self.known = {e: {} for e in ENGS}
        self.lastw = {}
        self.readers = {}
        self.dma_k = {e: 0 for e in ENGS}
        self.dsem_cnt = [0] * NDSEM

    def _deps(self, reads, writes):
        deps = []
        for t in list(reads) + list(writes):
            d = self.lastw.get(t)
            if d is not None:
                deps.append(d)
        for t in writes:
            deps.extend(self.readers.get(t, []))
        return deps

    def _record(self, dep, reads, writes):
        for t in writes:
            self.lastw[t] = dep
            self.readers[t] = []
        for t in reads:
            if t in writes:
                continue
            self.readers.setdefault(t, []).append(dep)

    def _waits(self, eng, deps, is_dma=False):
        need = {}
        for kind, key, val in deps:
            if kind == 'c' and key == eng and not is_dma and eng == 'pe':
                continue
            k = (kind, key)
            if self.known[eng].get(k, 0) >= val:
                continue
            need[k] = max(need.get(k, 0), val)
        for k, v in need.items():
            self.known[eng][k] = v
        return list(need.items())

    def op(self, eng, fn, reads=(), writes=()):
        deps = self._deps(reads, writes)
        waits = self._waits(eng, deps)
        self.cnt[eng] += 1
        dep = ('c', eng, self.cnt[eng])
        self.ops[eng].append((fn, waits, ('c', eng)))
        self._record(dep, reads, writes)

    def dma(self, eng, fn, reads=(), writes=()):
        deps = self._deps(reads, writes)
        base, n = DQ[eng]
        s = base + self.dma_k[eng] % n
        self.dma_k[eng] += 1
        if self.dsem_cnt[s] > 0:
            deps = list(deps) + [('d', s, self.dsem_cnt[s])]
        waits = self._waits(eng, deps, is_dma=True)
        self.dsem_cnt[s] += 16
        dep = ('d', s, self.dsem_cnt[s])
        self.ops[eng].append((fn, waits, ('d', s)))
        self._record(dep, reads, writes)

    def emit(self):
        nc = self.nc
        with contextlib.ExitStack() as st:
            csem = {e: st.enter_context(nc.semaphore('c_' + e)) for e in ENGS}
            dsem = [st.enter_context(nc.semaphore('d_%d' % i)) for i in range(NDSEM)]
            block = st.enter_context(nc.Block())

            def semof(k):
                return csem[k[1]] if k[0] == 'c' else dsem[k[1]]
            final_waits = [(('d', i), v) for i, v in enumerate(self.dsem_cnt) if v > 0]
            final_waits += [(('c', e), v) for e, v in self.cnt.items() if v > 0 and e != 'sp']

            def run(engname, eo):
                for fn, waits, inc in self.ops[engname]:
                    for k, v in waits:
                        eo.wait_ge(semof(k), v)
                    ins = fn(eo)
                    if inc[0] == 'c':
                        ins.then_inc(csem[inc[1]], 1)
                    else:
                        ins.then_inc(dsem[inc[1]], 16)
                if engname == 'sp':
                    for k, v in final_waits:
                        eo.wait_ge(semof(k), v)

            @block.tensor
            def _(e):
                run('pe', e)

            @block.scalar
            def _(e):
                run('act', e)

            @block.vector
            def _(e):
                run('dve', e)

            @block.gpsimd
            def _(e):
                run('pool', e)

            @block.sync
            def _(e):
                run('sp', e)


class RPool:
    def __init__(self, name, tensors):
        self.name, self.t, self.i = name, tensors, 0

    def get(self):
        i = self.i % len(self.t)
        self.i += 1
        return self.t[i], (self.name, i)


def build(n_st=NST, do_sample=True, debug=False, npool=NPOOL):
    nc = bass.Bass("TRN2", target_bir_lowering=False)
    dt_in = {}

    def din(name, shape, dt=F32):
        dt_in[name] = (shape, dt)
        return nc.dram_tensor(name, list(shape), dt, kind="ExternalInput").ap()

    def dout(name, shape):
        return nc.dram_tensor(name, list(shape), F32, kind="ExternalOutput").ap()

    xp = din("xp", [SEQ, D])
    xs = din("xs", [4, D])
    ck = din("ck", [npool, 128 * 512]) if do_sample else None
    cv = din("cv", [npool, 128 * 512]) if do_sample else None
    pt = din("pt", [128, 4], I32)
    st_re = din("st_re", [4, 32, 64])
    st_im = din("st_im", [4, 32, 64])
    st_conv = din("st_conv", [4, 3, 512])
    st_lru = din("st_lru", [4, 512])
    st_pool = din("st_pool", [4, 15, 512])
    norm_mix = din("norm_mix", [2, D])
    norm_ffn = din("norm_ffn", [2, D])
    norm_final = din("norm_final", [D])
    w_in_ab = din("w_in_ab", [D, 2048])
    w_out_ab = din("w_out_ab", [D, D])
    a_re = din("s5_a_re", [32, 64])
    a_im = din("s5_a_im", [32, 64])
    log_dt = din("s5_log_dt", [32])
    b_re = din("s5_b_re", [32, 64, 16])
    b_im = din("s5_b_im", [32, 64, 16])
    c_re = din("s5_c_re", [32, 16, 64])
    c_im = din("s5_c_im", [32, 16, 64])
    s5_d = din("s5_d", [512])
    w_glu = din("s5_w_glu", [512, 512])
    b_glu = din("s5_b_glu", [512])
    lq1 = din("diff_lq1", [64])
    lk1 = din("diff_lk1", [64])
    lq2 = din("diff_lq2", [64])
    lk2 = din("diff_lk2", [64])
    subln = din("diff_subln", [128])
    w_in_cd = din("w_in_cd", [D, 1536])
    w_out_cd = din("w_out_cd", [D, D])
    conv_w = din("conv_w", [4, 512])
    conv_b = din("conv_b", [512])
    lru_wa = din("lru_wa", [8, 64, 64])
    lru_ba = din("lru_ba", [512])
    lru_wx = din("lru_wx", [8, 64, 64])
    lru_bx = din("lru_bx", [512])
    lru_lam = din("lru_lambda", [512])
    pool_w = din("pool_w", [4, 128, 128])
    pool_scale = din("pool_scale", [512])
    wg = din("ffn_w_gate", [2, D, FFN])
    wu = din("ffn_w_up", [2, D, FFN])
    wd = din("ffn_w_down", [2, FFN, D])
    c_ident = din("c_ident", [128, 128])
    c_trie = din("c_trie", [128, 129])
    c_sel2 = din("c_sel2", [128, 128])
    c_gmask = din("c_gmask", [128, 8])
    c_iota = din("c_iota", [128, 640])
    c_part = din("c_part", [128, 1])
    c_bmask = din("c_bmask", [8, 512])
    c_oh84 = din("c_oh84", [8, 4, 4])
    c_oh44 = din("c_oh44", [4, 4, 128])
    c_oh14 = din("c_oh14", [1, 4, 4])
    c_alt = din("c_alt", [8, 1])

    y_p = dout("y_p", [SEQ, D])
    y_s = dout("y_s", [4, D])
    k_p = dout("k_p", [SEQ, 512])
    v_p = dout("v_p", [SEQ, 512])
    k_s = dout("k_s", [4, 512])
    v_s = dout("v_s", [4, 512])
    re_p = dout("re_p", [32, 64])
    im_p = dout("im_p", [32, 64])
    re_s = dout("re_s", [4, 32, 64])
    im_s = dout("im_s", [4, 32, 64])
    conv_p = dout("conv_p", [3, 512])
    conv_s = dout("conv_s", [4, 3, 512])
    lru_p = dout("lru_p", [512])
    lru_s = dout("lru_s", [4, 512])
    pool_p = dout("pool_p", [15, 512])
    pool_s = dout("pool_s", [4, 15, 512])
    kt_scr = nc.dram_tensor("kt_scr", [4, 128, SEQ], BF16, kind="Internal").ap()
    if debug:
        dbgR = dout("dbgR", [4, 512, D])
        dbgA = nc.dram_tensor("dbgA", [2, 8, 128, 512], BF16, kind="ExternalOutput").ap()
        dbgS = nc.dram_tensor("dbgS", [8, 128, 4], BF16, kind="ExternalOutput").ap()
        dbgRs = dout("dbgRs", [2, 4, D])

    P = Prog(nc)
    es = contextlib.ExitStack()
    uid = [0]

    def sb(shape, dt=F32, name=None):
        uid[0] += 1
        return es.enter_context(nc.sbuf_tensor(name or ("t%d" % uid[0]), list(shape), dt))

    def psum(shape, dt=F32):
        uid[0] += 1
        return es.enter_context(nc.psum_tensor("p%d" % uid[0], list(shape), dt))

    with es:
        fpool = RPool('f', [sb([128, 512]) for _ in range(10)])
        hpool = RPool('h', [sb([128, 512], BF16) for _ in range(8)])
        xpool = RPool('x', [sb([128, 512], BF16) for _ in range(4)])
        zpool = RPool('z', [sb([128, 128], BF16) for _ in range(4)])
        kpool = RPool('k', [sb([128, 512], BF16) for _ in range(2)])
        wpool = RPool('w', [sb([128, 8, 512], BF16) for _ in range(3)])
        spool = RPool('s', [sb([128, 8]) for _ in range(12)])
        pspool = RPool('ps', [psum([128, 512]) for _ in range(4)])
        pbpool = RPool('pb', [psum([128, 1024], BF16) for _ in range(1)])
        psO = [psum([128, 512]) for _ in range(2)]
        psL = psum([128, 512])

        R = sb([128, 4, D])
        aT = sb([128, 8, 512], BF16)
        QT = sb([128, 4, 512], BF16)
        junk = sb([128, D], BF16)
        gbuf = sb([128, D])

        def aTk(n):
            return ('aT', n)
        aT_all = [aTk(n) for n in range(4)]

        def tt(eng, out, a, b, op, reads, writes):
            P.op(eng, lambda e: e.tensor_tensor(out=out, in0=a, in1=b, op=op), reads, writes)

        def ts(eng, out, a, s1, op0, s2=None, op1=None, reads=(), writes=()):
            if op1 is None:
                P.op(eng, lambda e: e.tensor_scalar(out=out, in0=a, scalar1=s1, scalar2=None, op0=op0), reads, writes)
            else:
                P.op(eng, lambda e: e.tensor_scalar(out=out, in0=a, scalar1=s1, scalar2=s2, op0=op0, op1=op1), reads, writes)

        def stt(out, a, s, b, op0, op1, reads, writes):
            P.op('dve', lambda e: e.scalar_tensor_tensor(out=out, in0=a, scalar=s, in1=b, op0=op0, op1=op1), reads, writes)

        def act(out, a, func, reads, writes, scale=None, bias=None, accum=None):
            kw = {}
            if scale is not None:
                kw['scale'] = scale
            if bias is not None:
                kw['bias'] = bias
            if accum is not None:
                kw['accum_out'] = accum
            P.op('act', lambda e: e.activation(out=out, in_=a, func=func, **kw), reads, writes)

        def cp(eng, out, a, reads, writes):
            if eng == 'act':
                P.op('act', lambda e: e.activation(out=out, in_=a, func=AF.Copy), reads, writes)
            else:
                P.op(eng, lambda e: e.tensor_copy(out=out, in_=a), reads, writes)

        def recip(out, a, reads, writes):
            P.op('dve', lambda e: e.reciprocal(out=out, in_=a), reads, writes)

        def mm(out, lhsT, rhs, start, stop, reads, writes):
            P.op('pe', lambda e: e.matmul(out, lhsT=lhsT, rhs=rhs, start=start, stop=stop), reads, writes)

        def tr(out, in_, ident, reads, writes):
            P.op('pe', lambda e: e.transpose(out=out, in_=in_, identity=ident), reads, writes)

        def ld(eng, out, in_, writes, reads=()):
            P.dma(eng, lambda e: e.dma_start(out=out, in_=in_), reads=reads, writes=writes)

        def st_(out, in_, reads, writes=()):
            P.dma('sp', lambda e: e.dma_start(out=out, in_=in_), reads=reads, writes=writes)

        identf = sb([128, 128])
        identb = sb([128, 128], BF16)
        trie = sb([128, 129], BF16)
        sel2 = sb([128, 128])
        gmask = sb([128, 8])
        iota = sb([128, 640])
        partc = sb([128, 1])
        bmask = sb([8, 512])
        oh84 = sb([8, 4, 4])
        oh44 = sb([4, 4, 128])
        oh14 = sb([1, 4, 4])
        altc = sb([8, 1])
        ld('sp', identf[:], c_ident[:, :], ['identf'])
        ld('pool', identb[:], c_ident[:, :], ['identb'])
        ld('pool', trie[:], c_trie[:, :], ['trie'])
        ld('sp', sel2[:], c_sel2[:, :], ['sel2'])
        ld('sp', gmask[:], c_gmask[:, :], ['gmask'])
        ld('sp', iota[:], c_iota[:, :], ['iota'])
        ld('sp', partc[:], c_part[:, :], ['partc'])
        ld('sp', bmask[:], c_bmask[:, :], ['bmask'])
        ld('sp', oh84[:], c_oh84[:, :, :], ['oh84'])
        ld('sp', oh44[:], c_oh44[:, :, :], ['oh44'])
        ld('sp', oh14[:], c_oh14[:, :, :], ['oh14'])
        ld('sp', altc[:], c_alt[:, :], ['altc'])

        def bload(dst, src1d, tok, rows=128):
            ld('sp', dst, src1d.partition_broadcast(rows), [tok])

        gsrc = {'mix0': norm_mix[0, :], 'mix1': norm_mix[1, :], 'ffn0': norm_ffn[0, :], 'ffn1': norm_ffn[1, :], 'fin': norm_final[:]}

        def load_gain(key):
            bload(gbuf[:], gsrc[key], 'gbuf')

        def cols_from_rows(dst, src2d, nrows, reads_tok, wtok):
            t, kt = fpool.get()
            ld('sp', t[0:nrows, 0:128], src2d, [kt])
            ps, kps = pspool.get()
            tr(ps[:, 0:nrows], t[0:nrows, 0:128], identf[0:nrows, 0:nrows], [kt, 'identf'], [kps])
            cp('dve', dst, ps[:, 0:nrows], [kps], [wtok])

        def rows_out(dst2d, src, nrows, reads):
            ps, kps = pspool.get()
            tr(ps[0:nrows, 0:128], src, identf[:], list(reads) + ['identf'], [kps])
            t, kt = fpool.get()
            cp('dve', t[0:nrows, 0:128], ps[0:nrows, 0:128], [kps], [kt])
            st_(dst2d, t[0:nrows, 0:128], [kt])

        def range_reduce_sin(out, x, shift, rows, cols, reads, writes):
            t1, k1 = fpool.get()
            t2, k2 = fpool.get()
            a = t1[0:rows, 0:cols]
            b = t2[0:rows, 0:cols]
            ts('dve', a, x, shift, ALU.add, reads=reads, writes=[k1])
            ts('dve', b, a, 1.0 / TWO_PI, ALU.mult, MAGIC, ALU.add, reads=[k1], writes=[k2])
            ts('dve', b, b, -MAGIC, ALU.add, -TWO_PI, ALU.mult, reads=[k2], writes=[k2])
            tt('dve', a, a, b, ALU.add, [k1, k2], [k1])
            ts('dve', a, a, 3.1415925, ALU.min, -3.1415925, ALU.max, reads=[k1], writes=[k1])
            act(out, a, AF.Sin, [k1], writes)

        def gelu_tanh(out, x, rows, cols, reads, writes):
            t1, k1 = fpool.get()
            a = t1[0:rows, 0:cols]
            tt('dve', a, x, x, ALU.mult, reads, [k1])
            ts('dve', a, a, 0.044715, ALU.mult, 1.0, ALU.add, reads=[k1], writes=[k1])
            tt('dve', a, a, x, ALU.mult, list(reads) + [k1], [k1])
            act(a, a, AF.Sigmoid, [k1], [k1], scale=1.5957691216057308)
            tt('dve', out, a, x, ALU.mult, list(reads) + [k1], writes)

        lam_init0 = 0.8 - 0.6 * math.exp(-0.3 * 0)
        lamt = sb([128, 4])
        lq_t, klq_t = fpool.get()
        lq = lq_t[:].rearrange("p (a b) -> p a b", a=8)
        for i, src in enumerate([lq1, lk1, lq2, lk2]):
            P.dma('sp', lambda e, i=i, src=src: e.dma_start(out=lq[:, i, :], in_=src[:].partition_broadcast(128)), reads=[klq_t], writes=[klq_t])
        ltmp_t, kltmp_t = fpool.get()
        ltmp = ltmp_t[:].rearrange("p (a b) -> p a b", a=8)[:, 0:2, :]
        tt('dve', ltmp[:, 0, :], lq[:, 0, :], lq[:, 1, :], ALU.mult, [klq_t, kltmp_t], [kltmp_t])
        tt('dve', ltmp[:, 1, :], lq[:, 2, :], lq[:, 3, :], ALU.mult, [klq_t, kltmp_t], [kltmp_t])
        lsum = sb([128, 2])
        P.op('dve', lambda e: e.tensor_reduce(out=lsum[:], in_=ltmp, axis=AX.X, op=ALU.add), [kltmp_t], ['lsum'])
        act(lsum[:], lsum[:], AF.Exp, ['lsum'], ['lsum'])
        tt('dve', lamt[:, 0:1], lsum[:, 0:1], lsum[:, 1:2], ALU.subtract, ['lsum'], ['lamt'])
        ts('dve', lamt[:, 0:1], lamt[:, 0:1], lam_init0, ALU.add, reads=['lamt'], writes=['lamt'])
        ts('dve', lamt[:, 1:2], lamt[:, 0:1], -1.0, ALU.mult, reads=['lamt'], writes=['lamt'])
        ts('dve', lamt[:, 2:3], lamt[:, 1:2], -1.0, ALU.add, reads=['lamt'], writes=['lamt'])
        sublnB = sb([128, 128])
        bload(sublnB[:], subln[:], 'sublnB')
        ts('dve', sublnB[:], sublnB[:], 1.0 - lam_init0, ALU.mult, reads=['sublnB'], writes=['sublnB'])

        invB = sb([128, 32])
        act(invB[:], iota[:, 0:32], AF.Exp, ['iota'], ['invB'], scale=-math.log(10000.0) / 32.0)

        thg = sb([128, 32])
        rhg = sb([128, 32])
        dtg = sb([128, 32])
        are_g = sb([128, 32])
        aim_g = sb([128, 32])
        for (src, dst, nm) in ((a_re, are_g, 'are_g'), (a_im, aim_g, 'aim_g')):
            t, kt = fpool.get()
            ld('sp', t[0:32, 0:64], src[:, :], [kt])
            ld('sp', t[0:32, 64:128], src[:, :], [kt], reads=[kt])
            ps, kps = pspool.get()
            tr(ps[:, 0:32], t[0:32, 0:128], identf[0:32, 0:32], [kt, 'identf'], [kps])
            cp('dve', dst[:], ps[:, 0:32], [kps], [nm])
        bload(dtg[:], log_dt[:], 'dtg')
        act(dtg[:], dtg[:], AF.Exp, ['dtg'], ['dtg'])
        tt('dve', thg[:], aim_g[:], dtg[:], ALU.mult, ['aim_g', 'dtg'], ['thg'])
        tt('dve', rhg[:], are_g[:], dtg[:], ALU.mult, ['are_g', 'dtg'], ['rhg'])
        VR2 = sb([128, 32, 128], BF16)
        VI2 = sb([128, 32, 128], BF16)
        Vc = sb([128, 4, 32])
        for g0 in range(0, 32, 4):
            ang, ka = fpool.get()
            mag, km = fpool.get()
            a3 = ang[:].rearrange("p (g t) -> p g t", g=4)
            m3 = mag[:].rearrange("p (g t) -> p g t", g=4)
            io3 = iota[:, 0:128].unsqueeze(1).broadcast_to([128, 4, 128])
            tt('dve', a3, io3, thg[:, g0:g0 + 4].unsqueeze(2).broadcast_to([128, 4, 128]), ALU.mult, ['iota', 'thg'], [ka])
            tt('dve', m3, io3, rhg[:, g0:g0 + 4].unsqueeze(2).broadcast_to([128, 4, 128]), ALU.mult, ['iota', 'rhg'], [km])
            act(mag[:], mag[:], AF.Exp, [km], [km])
            sn, ksn = fpool.get()
            cs, kcs = fpool.get()
            range_reduce_sin(sn[:], ang[:], 0.0, 128, 512, [ka], [ksn])
            range_reduce_sin(cs[:], ang[:], math.pi / 2, 128, 512, [ka], [kcs])
            tt('dve', VR2[:, g0:g0 + 4, :], cs[:].rearrange("p (g t) -> p g t", g=4), m3, ALU.mult, [kcs, km], [('VR2', g0)])
            tt('dve', VI2[:, g0:g0 + 4, :], sn[:].rearrange("p (g t) -> p g t", g=4), m3, ALU.mult, [ksn, km], [('VI2', g0)])

        def cplx_pow(tval, dre, dim, wre, wim):
            ang, kang = fpool.get()
            ts('dve', ang[:, 0:32], thg[:], tval, ALU.mult, reads=['thg'], writes=[kang])
            mg, kmg = fpool.get()
            act(mg[:, 0:32], rhg[:], AF.Exp, ['rhg'], [kmg], scale=tval)
            sn, ksn = fpool.get()
            cs, kcs = fpool.get()
            range_reduce_sin(sn[:, 0:32], ang[:, 0:32], 0.0, 128, 32, [kang], [ksn])
            range_reduce_sin(cs[:, 0:32], ang[:, 0:32], math.pi / 2, 128, 32, [kang], [kcs])
            tt('dve', dre, cs[:, 0:32], mg[:, 0:32], ALU.mult, [kcs, kmg], [wre])
            tt('dve', dim, sn[:, 0:32], mg[:, 0:32], ALU.mult, [ksn, kmg], [wim])
        cplx_pow(127.0, Vc[:, 0, :], Vc[:, 1, :], ('Vc', 0), ('Vc', 1))
        cplx_pow(128.0, Vc[:, 2, :], Vc[:, 3, :], ('Vc', 2), ('Vc', 3))
        VcT = [('Vc', i) for i in range(4)]
        lb = sb([128, 2, 32])
        cplx_pow(1.0, lb[:, 0, :], lb[:, 1, :], 'lb0', 'lb1')
        fre = sb([128, 32])
        fim = sb([128, 32])
        if True:
            nre, knre = fpool.get()
            ts('dve', nre[:, 0:32], lb[:, 0, :], -1.0, ALU.add, reads=['lb0'], writes=[knre])
            den, kden = fpool.get()
            t1, kt1 = fpool.get()
            t2, kt2 = fpool.get()
            tt('dve', den[:, 0:32], are_g[:], are_g[:], ALU.mult, ['are_g'], [kden])
            tt('dve', t1[:, 0:32], aim_g[:], aim_g[:], ALU.mult, ['aim_g'], [kt1])
            tt('dve', den[:, 0:32], den[:, 0:32], t1[:, 0:32], ALU.add, [kden, kt1], [kden])
            recip(den[:, 0:32], den[:, 0:32], [kden], [kden])
            tt('dve', t1[:, 0:32], nre[:, 0:32], are_g[:], ALU.mult, [knre, 'are_g'], [kt1])
            tt('dve', t2[:, 0:32], lb[:, 1, :], aim_g[:], ALU.mult, ['lb1', 'aim_g'], [kt2])
            tt('dve', t1[:, 0:32], t1[:, 0:32], t2[:, 0:32], ALU.add, [kt1, kt2], [kt1])
            tt('dve', fre[:], t1[:, 0:32], den[:, 0:32], ALU.mult, [kt1, kden], ['fre'])
            tt('dve', t1[:, 0:32], lb[:, 1, :], are_g[:], ALU.mult, ['lb1', 'are_g'], [kt1])
            tt('dve', t2[:, 0:32], nre[:, 0:32], aim_g[:], ALU.mult, [knre, 'aim_g'], [kt2])
            tt('dve', t1[:, 0:32], t1[:, 0:32], t2[:, 0:32], ALU.subtract, [kt1, kt2], [kt1])
            tt('dve', fim[:], t1[:, 0:32], den[:, 0:32], ALU.mult, [kt1, kden], ['fim'])
        bnr, kbnr = fpool.get()
        bni, kbni = fpool.get()
        ld('sp', bnr[0:64, :].rearrange("p (g c) -> p g c", g=32), b_re.rearrange("g p c -> p g c"), [kbnr])
        ld('sp', bni[0:64, :].rearrange("p (g c) -> p g c", g=32), b_im.rearrange("g p c -> p g c"), [kbni])
        bst2, kbst2 = fpool.get()
        t1, kt1 = fpool.get()
        t2, kt2 = fpool.get()

        def f3(ap64):
            return ap64.unsqueeze(2).broadcast_to([64, 32, 16])

        def v3(ap):
            return ap.rearrange("p (g c) -> p g c", g=32)
        tt('dve', v3(t1[0:64, :]), v3(bnr[0:64, :]), f3(fre[0:64, :]), ALU.mult, [kbnr, 'fre'], [kt1])
        tt('dve', v3(t2[0:64, :]), v3(bni[0:64, :]), f3(fim[0:64, :]), ALU.mult, [kbni, 'fim'], [kt2])
        tt('dve', bst2[0:64, :], t1[0:64, :], t2[0:64, :], ALU.subtract, [kt1, kt2], [kbst2])
        tt('dve', v3(t1[0:64, :]), v3(bni[0:64, :]), f3(fre[0:64, :]), ALU.mult, [kbni, 'fre'], [kt1])
        tt('dve', v3(t2[0:64, :]), v3(bnr[0:64, :]), f3(fim[0:64, :]), ALU.mult, [kbnr, 'fim'], [kt2])
        tt('dve', t1[0:64, :], t1[0:64, :], t2[0:64, :], ALU.add, [kt1, kt2], [kt1])
        st_(bst2[64:128, :], t1[0:64, :], [kt1, kbst2], [kbst2])
        bst, kbst = hpool.get()
        cp('dve', bst[:], bst2[:], [kbst2], [kbst])
        BB = sb([128, 4, 8, 128], BF16)
        BBs = sb([128, 4, 8, 128], BF16)
        for q in range(4):
            pb, kpb = pbpool.get()
            tr(pb[:, 0:128], bst[:, 128 * q:128 * q + 128], identb[:], [kbst, 'identb'], [kpb])
            tq, ktq = hpool.get()
            cp('dve', tq[:, 0:128], pb[:, 0:128], [kpb], [ktq])
            tt('dve', BB[:, q, :, :], tq[:, 0:128].unsqueeze(1).broadcast_to([128, 8, 128]),
               gmask[:].unsqueeze(2).broadcast_to([128, 8, 128]), ALU.mult, [ktq, 'gmask'], [('BB', q)])
            cp('dve', BBs[:, q, :, 0:64], BB[:, q, :, 64:128], [('BB', q)], [('BBs0', q)])
            cp('dve', BBs[:, q, :, 64:128], BB[:, q, :, 0:64], [('BB', q)], [('BBs1', q)])
        BBT = [('BB', q) for q in range(4)]
        BBsT = [('BBs0', q) for q in range(4)] + [('BBs1', q) for q in range(4)]
        WA = sb([128, 32, 128], BF16)
        WB = sb([128, 32, 128], BF16)
        dtrow = sb([128, 32])
        bload(dtrow[:], log_dt[:], 'dtrow')
        act(dtrow[:], dtrow[:], AF.Exp, ['dtrow'], ['dtrow'])
        for g0 in range(0, 32, 8):
            ang, ka = fpool.get()
            mag, km = fpool.get()
            bload(ang[:], a_im[g0:g0 + 8, :].rearrange("g p -> (g p)"), ka)
            bload(mag[:], a_re[g0:g0 + 8, :].rearrange("g p -> (g p)"), km)
            d3 = dtrow[:, g0:g0 + 8].unsqueeze(2).broadcast_to([128, 8, 64])
            tt('dve', ang[:].rearrange("p (g s) -> p g s", g=8), ang[:].rearrange("p (g s) -> p g s", g=8), d3, ALU.mult, [ka, 'dtrow'], [ka])
            tt('dve', mag[:].rearrange("p (g s) -> p g s", g=8), mag[:].rearrange("p (g s) -> p g s", g=8), d3, ALU.mult, [km, 'dtrow'], [km])
            ts('dve', ang[:], ang[:], partc[:, 0:1], ALU.mult, reads=[ka, 'partc'], writes=[ka])
            ts('dve', mag[:], mag[:], partc[:, 0:1], ALU.mult, reads=[km, 'partc'], writes=[km])
            act(mag[:], mag[:], AF.Exp, [km], [km], scale=-1.0)
            sn, ksn = fpool.get()
            cs, kcs = fpool.get()
            range_reduce_sin(sn[:], ang[:], 0.0, 128, 512, [ka], [ksn])
            range_reduce_sin(cs[:], ang[:], math.pi / 2, 128, 512, [ka], [kcs])
            tt('dve', cs[:], cs[:], mag[:], ALU.mult, [kcs, km], [kcs])
            tt('dve', sn[:], sn[:], mag[:], ALU.mult, [ksn, km], [ksn])
            c3 = cs[:].rearrange("p (g s) -> p g s", g=8)
            s3 = sn[:].rearrange("p (g s) -> p g s", g=8)
            cp('dve', WA[:, g0:g0 + 8, 0:64], c3, [kcs], [('WA0', g0)])
            cp('dve', WA[:, g0:g0 + 8, 64:128], c3, [kcs], [('WA1', g0)])
            cp('dve', WB[:, g0:g0 + 8, 0:64], s3, [ksn], [('WB0', g0)])
            ts('dve', WB[:, g0:g0 + 8, 64:128], s3, -1.0, ALU.mult, reads=[ksn], writes=[('WB1', g0)])
        WT = [(n, g0) for n in ('WA0', 'WA1', 'WB0', 'WB1') for g0 in range(0, 32, 8)]
        CA = sb([128, 32, 16], BF16)
        CB = sb([128, 32, 16], BF16)
        for b in range(4):
            for (first, second, dstc, sgn_top, tok) in ((c_re, c_im, CA, 1.0, 'CA'), (c_im, c_re, CB, -1.0, 'CB')):
                t, kt = fpool.get()
                ld('sp', t[:, 0:64], first[8 * b:8 * b + 8, :, :].rearrange("g c p -> (g c) p"), [kt])
                ld('sp', t[:, 64:128], second[8 * b:8 * b + 8, :, :].rearrange("g c p -> (g c) p"), [kt], reads=[kt])
                ps, kps = pspool.get()
                tr(ps[:, 0:128], t[:, 0:128], identf[:], [kt, 'identf'], [kps])
                ts('dve', dstc[0:64, 8 * b:8 * b + 8, :], ps[0:64, 0:128].rearrange("p (g c) -> p g c", g=8), sgn_top, ALU.mult, reads=[kps], writes=[(tok, b, 0)])
                ts('dve', dstc[64:128, 8 * b:8 * b + 8, :], ps[64:128, 0:128].rearrange("p (g c) -> p g c", g=8), -1.0, ALU.mult, reads=[kps], writes=[(tok, b, 1)])
        CT = [(tok, b, h) for tok in ('CA', 'CB') for b in range(4) for h in range(2)]
        dB = sb([128, 512])
        bload(dB[:], s5_d[:], 'dB')
        bgB = sb([128, 512])
        bload(bgB[:], b_glu[:], 'bgB')
        H0 = sb([128, 32])
        Hn = sb([128, 32])
        Zc = sb([128, 4, 32])
        P.op('dve', lambda e: e.memset(H0[:], 0.0), [], ['H0'])

        masks = sb([128, 4, 512], BF16)
        ones_b = sb([128, 512], BF16)
        ones_f = sb([128, 1])
        P.op('dve', lambda e: e.memset(ones_b[:], 1.0), [], ['ones_b'])
        P.op('dve', lambda e: e.memset(ones_f[:], 1.0), [], ['ones_f'])
        for i in range(4):
            P.op('pool', lambda e, i=i: e.affine_select(out=masks[:, i, :], in_=ones_b[:], pattern=[[1, 512]],
                                                     compare_op=ALU.is_ge, fill=0.0, base=-128 * i,
                                                     channel_multiplier=-1), ['ones_b'], [('mask', i)])
        E2 = sb([128, 2, 2], BF16)
        P.op('dve', lambda e: e.memset(E2[:], 0.0), [], ['E2'])
        P.op('dve', lambda e: e.memset(E2[:, 0, 0:1], 1.0), ['E2'], ['E2'])
        P.op('dve', lambda e: e.memset(E2[:, 1, 1:2], 1.0), ['E2'], ['E2'])

        def rstd_col(rows, xap, xtok, n, eps):
            s, ks = spool.get()
            act(junk[0:rows, 0:n], xap, AF.Square, [xtok], ['junk', ks], accum=s[0:rows, 0:1])
            ts('dve', s[0:rows, 0:1], s[0:rows, 0:1], 1.0 / n, ALU.mult, eps, ALU.add, reads=[ks], writes=[ks])
            act(s[0:rows, 0:1], s[0:rows, 0:1], AF.Sqrt, [ks], [ks])
            recip(s[0:rows, 0:1], s[0:rows, 0:1], [ks], [ks])
            return s, ks

        def rmsnorm_T(rows, xtile, xtok, col0, dtok):
            s, ks = rstd_col(rows, xtile, xtok, D, 1e-6)
            h0, kh0 = hpool.get()
            h1, kh1 = hpool.get()
            stt(h0[0:rows, :], xtile[:, 0:512], s[0:rows, 0:1], gbuf[0:rows, 0:512], ALU.mult, ALU.mult, [xtok, ks, 'gbuf'], [kh0])
            stt(h1[0:rows, :], xtile[:, 512:1024], s[0:rows, 0:1], gbuf[0:rows, 512:1024], ALU.mult, ALU.mult, [xtok, ks, 'gbuf'], [kh1])
            pb, kpb = pbpool.get()
            for k in range(8):
                src = (h0 if k < 4 else h1)[0:rows, (k % 4) * 128:(k % 4) * 128 + 128]
                tr(pb[:, k * 128:k * 128 + rows], src, identb[0:rows, 0:rows], [kh0 if k < 4 else kh1, 'identb'], [kpb])
            cp('act', aT[:, :, col0:col0 + rows], pb[:].rearrange("p (k t) -> p k t", k=8)[:, :, 0:rows], [kpb], [dtok])

        def transpose_into(rows, src_bf, srck, ncb, dst3, cb0, col0, dtoks):
            pb, kpb = pbpool.get()
            for j in range(ncb):
                tr(pb[:, j * 128:j * 128 + rows], src_bf[:, j * 128:(j + 1) * 128], identb[0:rows, 0:rows], [srck, 'identb'], [kpb])
            cp('act', dst3[:, cb0:cb0 + ncb, col0:col0 + rows],
               pb[:, 0:ncb * 128].rearrange("p (k t) -> p k t", k=ncb)[:, :, 0:rows], [kpb], dtoks)

        def load_w(wdram, col0, ncols, kchunks=8):
            w, kw = wpool.get()
            ld('pool', w[:, 0:kchunks, 0:ncols], wdram[:, col0:col0 + ncols].rearrange("(k p) n -> p k n", p=128), [kw])
            return w, kw

        def lin_tok(rows, A3, acol0, atoks, w, kw, ncols, kchunks=8):
            ps, kps = pspool.get()
            for k in range(kchunks):
                mm(ps[0:rows, 0:ncols], A3[:, k, acol0:acol0 + rows], w[:, k, 0:ncols], k == 0, k == kchunks - 1,
                   list(atoks) + [kw], [kps])
            return ps, kps

        def lin_feat(ntok, A3, acol0, atoks, w, kw, wc0, kchunks=8):
            ps, kps = pspool.get()
            for k in range(kchunks):
                mm(ps[:, 0:ntok], w[:, k, wc0:wc0 + 128], A3[:, k, acol0:acol0 + ntok], k == 0, k == kchunks - 1,
                   list(atoks) + [kw], [kps])
            return ps, kps

        ropeC = sb([128, 64])
        ropeS = sb([128, 64])

        def rope_tables(rows, pos, postoks):
            ang, ka = fpool.get()
            ts('dve', ang[0:rows, 0:32], invB[0:rows, :], pos, ALU.mult, reads=['invB'] + list(postoks), writes=[ka])
            cc, kcc = ropeC, 'ropeC'
            ss, kss = ropeS, 'ropeS'
            range_reduce_sin(cc[0:rows, 0:32], ang[0:rows, 0:32], math.pi / 2, rows, 32, [ka], [kcc])
            range_reduce_sin(ss[0:rows, 32:64], ang[0:rows, 0:32], 0.0, rows, 32, [ka], [kss])
            cp('dve', cc[0:rows, 32:64], cc[0:rows, 0:32], [kcc], [kcc])
            ts('dve', ss[0:rows, 0:32], ss[0:rows, 32:64], -1.0, ALU.mult, reads=[kss], writes=[kss])
            return cc, kcc, ss, kss

        def rope_apply(rows, ps, kps, cc, kcc, ss, kss, out32, ko, scale=None):
            x3 = ps[0:rows, :].rearrange("p (h d) -> p h d", h=8)
            o3 = out32[0:rows, :].rearrange("p (h d) -> p h d", h=8)
            t, kt = fpool.get()
            t3 = t[0:rows, :].rearrange("p (h d) -> p h d", h=8)
            tt('dve', o3, x3, cc[0:rows, 0:64].unsqueeze(1).broadcast_to([rows, 8, 64]), ALU.mult, [kps, kcc], [ko])
            tt('dve', t3[:, :, 0:32], x3[:, :, 32:64], ss[0:rows, 0:32].unsqueeze(1).broadcast_to([rows, 8, 32]), ALU.mult, [kps, kss], [kt])
            tt('dve', t3[:, :, 32:64], x3[:, :, 0:32], ss[0:rows, 32:64].unsqueeze(1).broadcast_to([rows, 8, 32]), ALU.mult, [kps, kss, kt], [kt])
            tt('dve', out32[0:rows, :], out32[0:rows, :], t[0:rows, :], ALU.add, [ko, kt], [ko])
            if scale is not None:
                ts('dve', out32[0:rows, :], out32[0:rows, :], scale, ALU.mult, reads=[ko], writes=[ko])

        def ffn_block(layer, ntile, rows, Rtiles, gkey):
            load_gain(gkey)
            for n in range(ntile):
                rmsnorm_T(rows, Rtiles[n][0], Rtiles[n][1], n * 128, aTk(n))
            for j in range(6):
                c0 = j * 512
                nc_ = 512 if j < 5 else 256
                nb = nc_ // 128
                wgb, kwg = load_w(wg[layer], c0, nc_)
                wub, kwu = load_w(wu[layer], c0, nc_)
                wdbuf, kwd = wpool.get()
                wd3 = wdbuf[:].rearrange("p k n -> p (k n)")[:, 0:nb * 1024].rearrange("p (k n) -> p k n", k=nb)
                ld('pool', wd3, wd[layer][c0:c0 + nb * 128, :].rearrange("(k p) n -> p k n", p=128), [kwd])
                def stage_a(n):
                    pg, kpg = lin_tok(rows, aT, n * 128, [aTk(n)], wgb, kwg, nc_)
                    pu, kpu = lin_tok(rows, aT, n * 128, [aTk(n)], wub, kwu, nc_)
                    sg, ksg = fpool.get()
                    act(sg[0:rows, 0:nc_], pg[0:rows, 0:nc_], AF.Silu, [kpg], [ksg])
                    hb, khb = hpool.get()
                    tt('dve', hb[0:rows, 0:nc_], sg[0:rows, 0:nc_], pu[0:rows, 0:nc_], ALU.mult, [ksg, kpu], [khb])
                    return hb, khb

                def stage_b(n, hb, khb):
                    hTt, khT = hpool.get()
                    hT3 = hTt[:].rearrange("p (k t) -> p k t", k=4)
                    transpose_into(rows, hb[0:rows, :], khb, nb, hT3, 0, 0, [khT])
                    for c in range(2):
                        pd, kpd = pspool.get()
                        for k in range(nb):
                            mm(pd[0:rows, :], hT3[:, k, 0:rows], wd3[:, k, c * 512:(c + 1) * 512], k == 0, k == nb - 1, [khT, kwd], [kpd])
                        tt('dve', Rtiles[n][0][:, c * 512:(c + 1) * 512], Rtiles[n][0][:, c * 512:(c + 1) * 512], pd[0:rows, :], ALU.add,
                           [Rtiles[n][1], kpd], [Rtiles[n][1]])
                pend = None
                for n in range(ntile):
                    cur = stage_a(n)
                    if pend is not None:
                        stage_b(*pend)
                    pend = (n,) + cur
                stage_b(*pend)

        def out_proj(wdram, ntile, rows, Rtiles):
            for c in range(2):
                w, kw = load_w(wdram, c * 512, 512)
                for n in range(ntile):
                    ps, kps = lin_tok(rows, aT, n * 128, [aTk(n)], w, kw, 512)
                    tt('dve', Rtiles[n][0][:, c * 512:(c + 1) * 512], Rtiles[n][0][:, c * 512:(c + 1) * 512], ps[0:rows, :], ALU.add,
                       [Rtiles[n][1], kps], [Rtiles[n][1]])

        def final_norm_out(rows, xtile, xtok, dst):
            s, ks = rstd_col(rows, xtile, xtok, D, 1e-6)
            stt(xtile, xtile, s[0:rows, 0:1], gbuf[0:rows, :], ALU.mult, ALU.mult, [xtok, ks, 'gbuf'], [xtok])
            st_(dst, xtile, [xtok])

        def s5_chunk(rows, uT_fn, uTtoks, last, final_cb=None):
            for q in range(4):
                bu = []
                for (Btab, BT) in ((BB, BBT), (BBs, BBsT)):
                    for half in range(2):
                        ps, kps = pspool.get()
                        mm(ps[0:rows, :], uT_fn(q), Btab[:, q, 4 * half:4 * half + 4, :].rearrange("p g s -> p (g s)"), True, True,
                           list(uTtoks) + list(BT), [kps])
                        bu.append((ps, kps))
                X = []
                for half in range(2):
                    g0 = 8 * q + 4 * half
                    x1, kx1 = xpool.get()
                    x2, kx2 = xpool.get()
                    tt('dve', x1[0:rows, :], bu[half][0][0:rows, :], WA[0:rows, g0:g0 + 4, :].rearrange("p g s -> p (g s)"), ALU.mult,
                       [bu[half][1]] + WT, [kx1])
                    tt('dve', x2[0:rows, :], bu[2 + half][0][0:rows, :], WB[0:rows, g0:g0 + 4, :].rearrange("p g s -> p (g s)"), ALU.mult,
                       [bu[2 + half][1]] + WT, [kx2])
                    X.append((x1, kx1, x2, kx2))
                for gl in range(8):
                    g = 8 * q + gl
                    x1, kx1, x2, kx2 = X[gl // 4]
                    cs_ = (gl % 4) * 128
                    pg, kpg = pspool.get()
                    ncol = rows + 1
                    mm(pg[:, 0:ncol], x1[0:rows, cs_:cs_ + 128], trie[0:rows, 128 - rows:129], True, False, [kx1, 'trie'], [kpg])
                    mm(pg[:, 0:ncol], x2[0:rows, cs_:cs_ + 128], trie[0:rows, 128 - rows:129], False, True, [kx2, 'trie'], [kpg])
                    z1, kz1 = zpool.get()
                    z2, kz2 = zpool.get()
                    vk = [('VR2', g - g % 4), ('VI2', g - g % 4)]
                    stt(z1[:, 0:rows], pg[:, 0:rows], H0[:, g:g + 1], VR2[:, g, 0:rows], ALU.add, ALU.mult, [kpg, 'H0'] + vk, [kz1])
                    stt(z2[:, 0:rows], pg[:, 0:rows], H0[:, g:g + 1], VI2[:, g, 0:rows], ALU.add, ALU.mult, [kpg, 'H0'] + vk, [kz2])
                    if rows == 128:
                        stt(Zc[:, 0, g:g + 1], pg[:, 128:129], H0[:, g:g + 1], Vc[:, 2, g:g + 1], ALU.add, ALU.mult, [kpg, 'H0'] + VcT, [('Zc', 0, g)])
                        stt(Zc[:, 1, g:g + 1], pg[:, 128:129], H0[:, g:g + 1], Vc[:, 3, g:g + 1], ALU.add, ALU.mult, [kpg, 'H0'] + VcT, [('Zc', 1, g)])
                        if last:
                            stt(Zc[:, 2, g:g + 1], pg[:, 127:128], H0[:, g:g + 1], Vc[:, 0, g:g + 1], ALU.add, ALU.mult, [kpg, 'H0'] + VcT, [('Zc', 2, g)])
                            stt(Zc[:, 3, g:g + 1], pg[:, 127:128], H0[:, g:g + 1], Vc[:, 1, g:g + 1], ALU.add, ALU.mult, [kpg, 'H0'] + VcT, [('Zc', 3, g)])
                    else:
                        tt('dve', Hn[:, g:g + 1], pg[:, 0:1], H0[:, g:g + 1], ALU.add, [kpg, 'H0'], [('Hn', g)])
                    mm(psO[0][0:rows, 16 * g:16 * g + 16], z1[:, 0:rows], CA[:, g, :], True, False, [kz1] + CT, ['psO0'])
                    mm(psO[0][0:rows, 16 * g:16 * g + 16], z2[:, 0:rows], CB[:, g, :], False, True, [kz2] + CT, ['psO0'])
            if rows == 128:
                zr = [('Zc', 0, g) for g in range(32)] + [('Zc', 1, g) for g in range(32)]
                mm(psL[:, 0:32], identf[:], Zc[:, 0, :], True, False, zr + ['identf'], ['psL'])
                mm(psL[:, 0:32], sel2[:], Zc[:, 1, :], False, True, zr + ['sel2'], ['psL'])
                if last:
                    zr2 = [('Zc', 2, g) for g in range(32)] + [('Zc', 3, g) for g in range(32)]
                    mm(psL[:, 32:64], identf[:], Zc[:, 2, :], True, False, zr2 + ['identf'], ['psL'])
                    mm(psL[:, 32:64], sel2[:], Zc[:, 3, :], False, True, zr2 + ['sel2'], ['psL'])
                    final_cb()
                cp('dve', H0[:], psL[:, 0:32], ['psL'], ['H0'])

        def s5_post(rows, u_ps, ku, dst_tiles_fn):
            y32, ky = fpool.get()
            tt('dve', y32[0:rows, :], u_ps[0:rows, :], dB[0:rows, :], ALU.mult, [ku, 'dB'], [ky])
            tt('dve', y32[0:rows, :], y32[0:rows, :], psO[0][0:rows, :], ALU.add, [ky, 'psO0'], [ky])
            return y32, ky

        def glu_and_store(rows, y32, ky, col0, dtoks):
            z32, kz = fpool.get()
            gelu_tanh(z32[0:rows, :], y32[0:rows, :], rows, 512, [ky], [kz])
            zb, kzb = hpool.get()
            cp('act', zb[0:rows, :], z32[0:rows, :], [kz], [kzb])
            zT, kzT = hpool.get()
            zT3 = zT[:].rearrange("p (k t) -> p k t", k=4)
            transpose_into(rows, zb[0:rows, :], kzb, 4, zT3, 0, 0, [kzT])
            ps, kps = lin_tok(rows, zT3, 0, [kzT], wglu_sb, 'wglu', 512, kchunks=4)
            gg, kgg = fpool.get()
            tt('dve', gg[0:rows, :], ps[0:rows, :], bgB[0:rows, :], ALU.add, [kps, 'bgB'], [kgg])
            act(gg[0:rows, :], gg[0:rows, :], AF.Sigmoid, [kgg], [kgg])
            ob, kob = hpool.get()
            tt('dve', ob[0:rows, :], gg[0:rows, :], z32[0:rows, :], ALU.mult, [kgg, kz], [kob])
            transpose_into(rows, ob[0:rows, :], kob, 4, aT, 0, col0, dtoks)

        def subln_store(rows, o1, ko1, hp, col0, dtoks):
            s, ks = rstd_col(rows, o1, ko1, 128, 1e-5)
            ab, kab = hpool.get()
            stt(ab[0:rows, 0:128], o1, s[0:rows, 0:1], sublnB[0:rows, :], ALU.mult, ALU.mult, [ko1, ks, 'sublnB'], [kab])
            transpose_into(rows, ab[0:rows, :], kab, 1, aT, 4 + hp, col0, dtoks)

        wglu_sb = sb([128, 4, 512], BF16)
        ld('pool', wglu_sb[:], w_glu.rearrange("(k p) n -> p k n", p=128), ['wglu'])
        cdbuf = [sb([128, 4, 512]), sb([128, 4, 515]), sb([128, 4, 527])]
        cwc = sb([128, 4, 4])
        for j in range(4):
            cols_from_rows(cwc[:, j, :], conv_w[j, :].rearrange("(b p) -> b p", p=128), 4, [], ('cw', j))
        cwT = [('cw', j) for j in range(4)]
        cbc = sb([128, 4]); cols_from_rows(cbc[:], conv_b.rearrange("(b p) -> b p", p=128), 4, [], 'cbc')
        bac = sb([128, 4]); cols_from_rows(bac[:], lru_ba.rearrange("(b p) -> b p", p=128), 4, [], 'bac')
        bxc = sb([128, 4]); cols_from_rows(bxc[:], lru_bx.rearrange("(b p) -> b p", p=128), 4, [], 'bxc')
        lamc = sb([128, 4]); cols_from_rows(lamc[:], lru_lam.rearrange("(b p) -> b p", p=128), 4, [], 'lamc')
        pscc = sb([128, 4]); cols_from_rows(pscc[:], pool_scale.rearrange("(b p) -> b p", p=128), 4, [], 'pscc')
        act(lamc[:], lamc[:], AF.Exp, ['lamc'], ['lamc'], scale=-1.0)
        act(lamc[:], lamc[:], AF.Ln, ['lamc'], ['lamc'], bias=1.0)
        ts('dve', lamc[:], lamc[:], -8.0, ALU.mult, reads=['lamc'], writes=['lamc'])
        WAb = sb([128, 4, 128], BF16)
        WXb = sb([128, 4, 128], BF16)
        P.op('dve', lambda e: e.memset(WAb[:], 0.0), [], ['WAb'])
        P.op('dve', lambda e: e.memset(WXb[:], 0.0), [], ['WXb'])
        for blk in range(4):
            for h in range(2):
                ld('pool', WAb[64 * h:64 * h + 64, blk, 64 * h:64 * h + 64], lru_wa[2 * blk + h, :, :], ['WAb'], reads=['WAb'])
                ld('pool', WXb[64 * h:64 * h + 64, blk, 64 * h:64 * h + 64], lru_wx[2 * blk + h, :, :], ['WXb'], reads=['WXb'])
        PWb = sb([128, 4, 128], BF16)
        ld('pool', PWb[:], pool_w.rearrange("g c d -> c g d"), ['PWb'])

        def lru_gates(ntok, xc, kxc, blk):
            xcb, kxcb = hpool.get()
            cp('act', xcb[:, 0:ntok], xc, [kxc], [kxcb])
            pr, kpr = pspool.get()
            mm(pr[:, 0:ntok], WAb[:, blk, :], xcb[:, 0:ntok], True, True, ['WAb', kxcb], [kpr])
            pi_, kpi = pspool.get()
            mm(pi_[:, 0:ntok], WXb[:, blk, :], xcb[:, 0:ntok], True, True, ['WXb', kxcb], [kpi])
            rr, krr = fpool.get()
            ii, kii = fpool.get()
            act(rr[:, 0:ntok], pr[:, 0:ntok], AF.Sigmoid, [kpr, 'bac'], [krr], bias=bac[:, blk:blk + 1])
            act(ii[:, 0:ntok], pi_[:, 0:ntok], AF.Sigmoid, [kpi, 'bxc'], [kii], bias=bxc[:, blk:blk + 1])
            aa, kaa = fpool.get()
            act(aa[:, 0:ntok], rr[:, 0:ntok], AF.Exp, [krr, 'lamc'], [kaa], scale=lamc[:, blk:blk + 1])
            bb, kbb = fpool.get()
            tt('dve', bb[:, 0:ntok], aa[:, 0:ntok], aa[:, 0:ntok], ALU.mult, [kaa], [kbb])
            ts('dve', bb[:, 0:ntok], bb[:, 0:ntok], -1.0, ALU.mult, 1.0, ALU.add, reads=[kbb], writes=[kbb])
            ts('dve', bb[:, 0:ntok], bb[:, 0:ntok], 0.0, ALU.max, reads=[kbb], writes=[kbb])
            act(bb[:, 0:ntok], bb[:, 0:ntok], AF.Sqrt, [kbb], [kbb])
            tt('dve', ii[:, 0:ntok], ii[:, 0:ntok], xc, ALU.mult, [kii, kxc], [kii])
            tt('dve', bb[:, 0:ntok], bb[:, 0:ntok], ii[:, 0:ntok], ALU.mult, [kbb, kii], [kbb])
            return aa, kaa, bb, kbb

        uT = sb([128, 4, 512], BF16)
        xlh = sb([128, 4, 3])
        xph = sb([128, 4, 15])
        hprev = sb([128, 4])
        bigA = sb([128, 528])
        bigB = sb([128, 528])
        P.op('dve', lambda e: e.memset(xlh[:], 0.0), [], ['xlh'])
        P.op('dve', lambda e: e.memset(xph[:], 0.0), [], ['xph'])
        P.op('dve', lambda e: e.memset(hprev[:], 0.0), [], ['hprev'])

        for st in range(n_st):
            Rt = [(R[:, n, :], ('R', n)) for n in range(4)]
            t0 = st * 512
            for n in range(4):
                ld('sp', R[:, n, :], xp[t0 + n * 128:t0 + n * 128 + 128, :], [('R', n)])
            load_gain('mix0')
            for n in range(4):
                rmsnorm_T(128, R[:, n, :], ('R', n), n * 128, aTk(n))
            wq, kwq = load_w(w_in_ab[:, :], 512, 512)
            wk, kwk = load_w(w_in_ab[:, :], 1024, 512)
            for n in range(4):
                poscol, kpos = spool.get()
                ts('dve', poscol[:, 0:1], partc[:], float(t0 + n * 128), ALU.add, reads=['partc'], writes=[kpos])
                cc, kcc, ss, kss = rope_tables(128, poscol[:, 0:1], [kpos])
                ps, kps = lin_tok(128, aT, n * 128, [aTk(n)], wq, kwq, 512)
                q32, kq32 = fpool.get()
                rope_apply(128, ps, kps, cc, kcc, ss, kss, q32, kq32, scale=0.125)
                qb, kqb = hpool.get()
                cp('act', qb[:], q32[:], [kq32], [kqb])
                transpose_into(128, qb, kqb, 4, QT, 0, n * 128, [('QT', n)])
                ps, kps = lin_tok(128, aT, n * 128, [aTk(n)], wk, kwk, 512)
                k32, kk32 = fpool.get()
                rope_apply(128, ps, kps, cc, kcc, ss, kss, k32, kk32)
                st_(k_p[t0 + n * 128:t0 + n * 128 + 128, :], k32[:], [kk32])
                kb, kkb = hpool.get()
                cp('act', kb[:], k32[:], [kk32], [kkb])
                ktt, kktt = hpool.get()
                kt3 = ktt[:].rearrange("p (k t) -> p k t", k=4)
                transpose_into(128, kb, kkb, 4, kt3, 0, 0, [kktt])
                st_(kt_scr[:, :, t0 + n * 128:t0 + n * 128 + 128].rearrange("h p t -> p h t"), kt3, [kktt], [('ktscr', st * 4 + n)])
            wv, kwv = load_w(w_in_ab[:, :], 1536, 512)
            for n in range(4):
                ps, kps = lin_tok(128, aT, n * 128, [aTk(n)], wv, kwv, 512)
                v32, kv32 = fpool.get()
                cp('act', v32[:], ps[:], [kps], [kv32])
                st_(v_p[t0 + n * 128:t0 + n * 128 + 128, :], v32[:], [kv32], [('vout', st * 4 + n)])
            wu_, kwu_ = load_w(w_in_ab[:, :], 0, 512)
            for q in range(4):
                ps, kps = lin_feat(512, aT, 0, aT_all, wu_, kwu_, q * 128)
                cp('act', uT[:, q, :], ps[:], [kps], [('uT', q)])
            for n in range(4):
                last = (st == n_st - 1 and n == 3)

                def fin():
                    hf, khf = fpool.get()
                    cp('dve', hf[:, 0:32], psL[:, 32:64], ['psL'], [khf])
                    ps, kps = pspool.get()
                    tr(ps[0:32, 0:128], hf[:, 0:32], identf[:], [khf, 'identf'], [kps])
                    o, ko = fpool.get()
                    cp('dve', o[0:32, 0:128], ps[0:32, 0:128], [kps], [ko])
                    st_(re_p[:, :], o[0:32, 0:64], [ko])
                    st_(im_p[:, :], o[0:32, 64:128], [ko])
                pu_, kpu_ = lin_tok(128, aT, n * 128, [aTk(n)], wu_, kwu_, 512)
                ut, kut = fpool.get()
                cp('act', ut[:], pu_[:], [kpu_], [kut])
                s5_chunk(128, lambda q, n=n: uT[:, q, n * 128:n * 128 + 128], [('uT', q) for q in range(4)], last, fin)
                y32, ky = s5_post(128, ut, kut, None)
                glu_and_store(128, y32, ky, n * 128, [aTk(n)])
            nkt = 4 * (st + 1)
            for hp in range(4):
                state = {}

                def stage_s(kt, j, hp=hp, state=state):
                    kg = kt - kt % 4
                    if kt % 4 == 0 and j == 0:
                        kts, kkts = kpool.get()
                        ld('sp', kts[:, 0:512], kt_scr[hp, :, kg * 128:kg * 128 + 512], [kkts], reads=[('ktscr', kg + i) for i in range(4)])
                        state['kts'] = (kts, kkts)
                    kts, kkts = state['kts']
                    if j == 0:
                        vt, kvt = hpool.get()
                        ld('pool', vt[:, 0:128], v_p[kt * 128:kt * 128 + 128, hp * 128:hp * 128 + 128], [kvt], reads=[('vout', kt)])
                        state['vt'] = (vt, kvt)
                    vt, kvt = state['vt']
                    ps, kps = pspool.get()
                    mm(ps[:], kts[64 * j:64 * j + 64, (kt - kg) * 128:(kt - kg) * 128 + 128], QT[64 * j:64 * j + 64, hp, :], True, True,
                       [kkts] + [('QT', n) for n in range(4)], [kps])
                    pT, kpT = hpool.get()
                    act(pT[:], ps[:], AF.Exp, [kps], [kpT])
                    di = kt - 4 * st
                    if di >= 0:
                        tt('pool', pT[:], pT[:], masks[:, di, :], ALU.mult, [kpT, ('mask', di)], [kpT])
                    return (kt, j, vt, kvt, pT, kpT)

                def stage_pv(kt, j, vt, kvt, pT, kpT):
                    mm(psO[j][:], vt[:, 0:128], pT[:], kt == 0, kt == nkt - 1, [kvt, kpT], ['psO%d' % j])
                    mm(psL[0:2, 0:512], E2[:, j, :], pT[:], kt == 0 and j == 0, kt == nkt - 1 and j == 1, [kpT, 'E2'], ['psL'])
                its = [(kt, j) for kt in range(nkt) for j in range(2)]
                q_ = []
                for (kt, j) in its:
                    q_.append(stage_s(kt, j))
                    if len(q_) > 2:
                        stage_pv(*q_.pop(0))
                while q_:
                    stage_pv(*q_.pop(0))
                lsb, klsb = fpool.get()
                cp('dve', lsb[0:2, :], psL[0:2, 0:512], ['psL'], [klsb])
                osb = []
                for j in range(2):
                    o, ko = fpool.get()
                    cp('act', o[:], psO[j][:], ['psO%d' % j], [ko])
                    osb.append((o, ko))
                for n in range(4):
                    pl, kpl = pspool.get()
                    tr(pl[:, 0:2], lsb[0:2, n * 128:n * 128 + 128], identf[0:2, 0:2], [klsb, 'identf'], [kpl])
                    rl, krl = spool.get()
                    recip(rl[:, 0:2], pl[:, 0:2], [kpl], [krl])
                    tt('dve', rl[:, 1:2], rl[:, 1:2], lamt[:, 1:2], ALU.mult, [krl, 'lamt'], [krl])
                    po = []
                    for j in range(2):
                        pt_, kpt_ = pspool.get()
                        tr(pt_[:, 0:128], osb[j][0][:, n * 128:n * 128 + 128], identf[:], [osb[j][1], 'identf'], [kpt_])
                        po.append((pt_, kpt_))
                    o1, ko1 = fpool.get()
                    ts('dve', o1[:, 0:128], po[0][0][:, 0:128], rl[:, 0:1], ALU.mult, reads=[po[0][1], krl], writes=[ko1])
                    stt(o1[:, 0:128], po[1][0][:, 0:128], rl[:, 1:2], o1[:, 0:128], ALU.mult, ALU.add, [po[1][1], krl, ko1], [ko1])
                    subln_store(128, o1[:, 0:128], ko1, hp, n * 128, [aTk(n)])
            def dump_aT(i):
                if debug and st == 0:
                    P.dma('sp', lambda e: e.dma_start(out=dbgA[i].rearrange("k p t -> p k t"), in_=aT[:]), reads=aT_all)

            def dump_R(i):
                if debug and st == 0:
                    for n in range(4):
                        st_(dbgR[i, n * 128:n * 128 + 128, :], R[:, n, :], [('R', n)])
            dump_aT(0)
            out_proj(w_out_ab[:, :], 4, 128, Rt)
            dump_R(0)
            ffn_block(0, 4, 128, Rt, 'ffn0')
            dump_R(1)
            load_gain('mix1')
            for n in range(4):
                rmsnorm_T(128, R[:, n, :], ('R', n), n * 128, aTk(n))
            ext = {}
            for part, halo, hbuf, hk in ((0, 0, None, None), (1, 3, xlh, 'xlh'), (2, 15, xph, 'xph')):
                w, kw = load_w(w_in_cd[:, :], part * 512, 512)
                for blk in range(4):
                    ps, kps = lin_feat(512, aT, 0, aT_all, w, kw, blk * 128)
                    dstt = cdbuf[part][:, blk, halo:halo + 512]
                    cp('act', dstt, ps[:], [kps], [('cd', part, blk)])
                    if halo:
                        cp('dve', cdbuf[part][:, blk, 0:halo], hbuf[:, blk, :], [hk], [('cdh', part, blk)])
            for blk in range(4):
                xl = cdbuf[1][:, blk, :]
                xlk = [('cd', 1, blk), ('cdh', 1, blk)]
                xc, kxc = fpool.get()
                ts('dve', xc[:], xl[:, 0:512], cwc[:, 0, blk:blk + 1], ALU.mult, cbc[:, blk:blk + 1], ALU.add, reads=xlk + cwT + ['cbc'], writes=[kxc])
                for j in range(1, 4):
                    stt(xc[:], xl[:, j:j + 512], cwc[:, j, blk:blk + 1], xc[:], ALU.mult, ALU.add, xlk + cwT + [kxc], [kxc])
                aa, kaa, bb, kbb = lru_gates(512, xc[:], kxc, blk)
                hs, khs = fpool.get()
                P.op('dve', lambda e, hs=hs, aa=aa, bb=bb, blk=blk: e.tensor_tensor_scan(out=hs[:], data0=aa[:], data1=bb[:], initial=hprev[:, blk:blk + 1],
                                                                                  op0=ALU.mult, op1=ALU.add), [kaa, kbb, 'hprev'], [khs])
                cp('dve', hprev[:, blk:blk + 1], hs[:, 511:512], [khs], ['hprev'])
                gl_, kgl = fpool.get()
                gelu_tanh(gl_[:], cdbuf[0][:, blk, :], 128, 512, [('cd', 0, blk)], [kgl])
                tt('dve', aT[:, blk, :], gl_[:], hs[:], ALU.mult, [kgl, khs], aT_all)
                wdw = 2 ** (blk + 1)
                xpv = cdbuf[2][:, blk, :]
                xpk = [('cd', 2, blk), ('cdh', 2, blk)]
                src_ap, src_k, width = xpv, xpk, 527
                m = 1
                flip = 0
                while m < wdw:
                    nw = width - m
                    big = bigA if flip == 0 else bigB
                    kbig = 'bigA' if flip == 0 else 'bigB'
                    tt('dve', big[:, 0:nw], src_ap[:, m:m + nw], src_ap[:, 0:nw], ALU.add, list(src_k), [kbig])
                    src_ap, src_k, width = big, [kbig], nw
                    m *= 2
                    flip ^= 1
                o0 = 15 - (wdw - 1)
                pl_, kpl_ = fpool.get()
                if st == 0:
                    ts('dve', pl_[:], iota[:, 0:512], 1.0, ALU.add, float(wdw), ALU.min, reads=['iota'], writes=[kpl_])
                    recip(pl_[:], pl_[:], [kpl_], [kpl_])
                    tt('dve', pl_[:], pl_[:], src_ap[:, o0:o0 + 512], ALU.mult, [kpl_] + list(src_k), [kpl_])
                else:
                    ts('dve', pl_[:], src_ap[:, o0:o0 + 512], 1.0 / wdw, ALU.mult, reads=list(src_k), writes=[kpl_])
                plb, kplb = hpool.get()
                tt('dve', plb[:], pl_[:], xpv[:, 15:527], ALU.subtract, [kpl_] + xpk, [kplb])
                pp, kpp = pspool.get()
                mm(pp[:], PWb[:, blk, :], plb[:], True, True, ['PWb', kplb], [kpp])
                ts('dve', aT[:, 4 + blk, :], pp[:], pscc[:, blk:blk + 1], ALU.mult, reads=[kpp, 'pscc'], writes=aT_all)
                cp('dve', xlh[:, blk, :], xl[:, 512:515], xlk, ['xlh'])
                cp('dve', xph[:, blk, :], xpv[:, 512:527], xpk, ['xph'])
                if st == n_st - 1:
                    rows_out(conv_p[:, blk * 128:blk * 128 + 128], xl[:, 512:515], 3, xlk)
                    rows_out(pool_p[:, blk * 128:blk * 128 + 128], xpv[:, 512:527], 15, xpk)
            if st == n_st - 1:
                rows_out(lru_p.rearrange("(b p) -> b p", p=128), hprev[:], 4, ['hprev'])
            dump_aT(1)
            out_proj(w_out_cd[:, :], 4, 128, Rt)
            dump_R(2)
            ffn_block(1, 4, 128, Rt, 'ffn1')
            dump_R(3)
            load_gain('fin')
            for n in range(4):
                final_norm_out(128, R[:, n, :], ('R', n), y_p[t0 + n * 128:t0 + n * 128 + 128, :])

        if do_sample:
            Rs = R[0:4, 0, :]
            RsK = ('R', 0)
            Rts = [(Rs, RsK)]
            ld('sp', Rs, xs[:, :], [RsK])
            load_gain('mix0')
            rmsnorm_T(4, Rs, RsK, 0, aTk(0))
            cc, kcc, ss, kss = rope_tables(4, float(PAST), [])
            q32s = cdbuf[0][0:4, 0, :]
            k32s = cdbuf[0][0:4, 1, :]
            v32s = cdbuf[0][0:4, 2, :]
            u32s = cdbuf[0][0:4, 3, :]
            uTs = sb([128, 4, 4], BF16)
            w, kw = load_w(w_in_ab[:, :], 512, 512)
            ps, kps = lin_tok(4, aT, 0, [aTk(0)], w, kw, 512)
            rope_apply(4, ps, kps, cc, kcc, ss, kss, q32s, 'q32s', scale=0.125)
            w, kw = load_w(w_in_ab[:, :], 1024, 512)
            ps, kps = lin_tok(4, aT, 0, [aTk(0)], w, kw, 512)
            rope_apply(4, ps, kps, cc, kcc, ss, kss, k32s, 'k32s')
            st_(k_s[:, :], k32s, ['k32s'])
            w, kw = load_w(w_in_ab[:, :], 1536, 512)
            ps, kps = lin_tok(4, aT, 0, [aTk(0)], w, kw, 512)
            cp('act', v32s, ps[0:4, :], [kps], ['v32s'])
            st_(v_s[:, :], v32s, ['v32s'])
            w, kw = load_w(w_in_ab[:, :], 0, 512)
            ps, kps = lin_tok(4, aT, 0, [aTk(0)], w, kw, 512)
            cp('act', u32s, ps[0:4, :], [kps], ['u32s'])
            for q in range(4):
                ps, kps = lin_feat(4, aT, 0, [aTk(0)], w, kw, q * 128)
                cp('act', uTs[:, q, :], ps[:, 0:4], [kps], [('uTs', q)])
            for r in range(4):
                hin, khin = fpool.get()
                ld('sp', hin[0:32, 0:64], st_re[r, :, :], [khin])
                ld('sp', hin[0:32, 64:128], st_im[r, :, :], [khin], reads=[khin])
                ps, kps = pspool.get()
                tr(ps[:, 0:32], hin[0:32, 0:128], identf[0:32, 0:32], [khin, 'identf'], [kps])
                z1, kz1 = fpool.get()
                z2, kz2 = fpool.get()
                tt('dve', z1[:, 0:32], ps[:, 0:32], lb[:, 0, :], ALU.mult, [kps, 'lb0'], [kz1])
                tt('dve', z2[:, 0:32], ps[:, 0:32], lb[:, 1, :], ALU.mult, [kps, 'lb1'], [kz2])
                mm(psL[:, 0:32], identf[:], z1[:, 0:32], True, False, [kz1, 'identf'], ['psL'])
                mm(psL[:, 0:32], sel2[:], z2[:, 0:32], False, True, [kz2, 'sel2'], ['psL'])
                cp('dve', H0[:], psL[:, 0:32], ['psL'], ['H0'])
                s5_chunk(1, lambda q, r=r: uTs[:, q, r:r + 1], [('uTs', q) for q in range(4)], False)
                ps, kps = pspool.get()
                tr(ps[0:32, 0:128], Hn[:], identf[:], [('Hn', g) for g in range(32)] + ['identf'], [kps])
                o, ko = fpool.get()
                cp('dve', o[0:32, 0:128], ps[0:32, 0:128], [kps], [ko])
                st_(re_s[r, :, :], o[0:32, 0:64], [ko])
                st_(im_s[r, :, :], o[0:32, 64:128], [ko])
                yr, kyr = fpool.get()
                cp('dve', yr[0:1, :], psO[0][0:1, :], ['psO0'], [kyr])
                mm(psO[1][0:4, :], oh14[0:1, r, :], yr[0:1, :], r == 0, r == 3, [kyr, 'oh14'], ['psO1'])
            y32, ky = fpool.get()
            tt('dve', y32[0:4, :], u32s, dB[0:4, :], ALU.mult, ['u32s', 'dB'], [ky])
            tt('dve', y32[0:4, :], y32[0:4, :], psO[1][0:4, :], ALU.add, [ky, 'psO1'], [ky])
            glu_and_store(4, y32, ky, 0, [aTk(0)])
            it = sb([128, 4], I32)
            ld('sp', it[:], pt[:, :], ['it'])
            itf = sb([128, 4])
            ts('dve', itf[:], it[:], 128.0, ALU.mult, reads=['it'], writes=['itf'])
            ipool = RPool('ix', [sb([128, 1], I32) for _ in range(4)])
            ckrows = ck.rearrange("n (t c) -> (n t) c", c=512)
            cvrows = cv.rearrange("n (t c) -> (n t) c", c=512)
            pself = sb([4, 8])
            t, kt = fpool.get()
            tt('dve', t[0:4, :], q32s, k32s, ALU.mult, ['q32s', 'k32s'], [kt])
            P.op('dve', lambda e: e.tensor_reduce(out=pself[:], in_=t[0:4, :].rearrange("p (h d) -> p h d", h=8), axis=AX.X, op=ALU.add), [kt], ['pself'])
            act(pself[:], pself[:], AF.Exp, ['pself'], ['pself'])
            sg8 = sb([8, 1])
            ts('dve', sg8[:], altc[:], lamt[0:8, 2:3], ALU.mult, 1.0, ALU.add, reads=['altc', 'lamt'], writes=['sg8'])
            qB = cdbuf[1][:, 0, 0:512]
            for r in range(4):
                ps, kps = pspool.get()
                mm(ps[:], oh44[0:4, r, :], q32s, True, True, ['oh44', 'q32s'], [kps])
                cp('act', qB, ps[:], [kps], ['qB'])
                def stage_g(tk, r=r):
                    Kc, kKc = fpool.get()
                    Vc_, kVc = fpool.get()
                    ix, kix = ipool.get()
                    ts('dve', ix[:], itf[:, r:r + 1], float(tk), ALU.add, reads=['itf'], writes=[kix])
                    P.dma('pool', lambda e, Kc=Kc, ix=ix: e.indirect_dma_start(
                        out=Kc[:], out_offset=None, in_=ckrows[:, :],
                        in_offset=bass.IndirectOffsetOnAxis(ap=ix[:, :], axis=0)),
                        reads=[kix], writes=[kKc])
                    P.dma('pool', lambda e, Vc_=Vc_, ix=ix: e.indirect_dma_start(
                        out=Vc_[:], out_offset=None, in_=cvrows[:, :],
                        in_offset=bass.IndirectOffsetOnAxis(ap=ix[:, :], axis=0)),
                        reads=[kix], writes=[kVc])
                    return (tk, Kc, kKc, Vc_, kVc)

                def stage_c(tk, Kc, kKc, Vc_, kVc):
                    pr, kpr = fpool.get()
                    tt('dve', pr[:], Kc[:], qB, ALU.mult, [kKc, 'qB'], [kpr])
                    sc, ksc = spool.get()
                    P.op('dve', lambda e, sc=sc, pr=pr: e.tensor_reduce(out=sc[:, 0:8], in_=pr[:].rearrange("p (h d) -> p h d", h=8), axis=AX.X, op=ALU.add), [kpr], [ksc])
                    act(sc[:, 0:8], sc[:, 0:8], AF.Exp, [ksc], [ksc])
                    mm(psO[0][0:8, :], sc[:, 0:8], Vc_[:], tk == 0, False, [ksc, kVc], ['psO0'])
                    mm(psL[0:1, 0:8], ones_f[:, 0:1], sc[:, 0:8], tk == 0, False, [ksc, 'ones_f'], ['psL'])
                gq = []
                for tk in range(128):
                    gq.append(stage_g(tk))
                    if len(gq) > 2:
                        stage_c(*gq.pop(0))
                while gq:
                    stage_c(*gq.pop(0))
                pm, kpm = spool.get()
                ts('dve', pm[0:4, 0:8], pself[:], oh44[0:4, r, 0:1], ALU.mult, reads=['pself', 'oh44'], writes=[kpm])
                mm(psO[0][0:8, :], pm[0:4, 0:8], v32s, False, True, [kpm, 'v32s'], ['psO0'])
                mm(psL[0:1, 0:8], oh44[0:4, r, 0:1], pself[:], False, True, ['pself', 'oh44'], ['psL'])
                lrow, klrow = fpool.get()
                cp('dve', lrow[0:1, 0:8], psL[0:1, 0:8], ['psL'], [klrow])
                ps, kps = pspool.get()
                tr(ps[0:8, 0:1], lrow[0:1, 0:8], identf[0:1, 0:1], [klrow, 'identf'], [kps])
                cv8, kcv8 = spool.get()
                recip(cv8[0:8, 0:1], ps[0:8, 0:1], [kps], [kcv8])
                tt('dve', cv8[0:8, 0:1], cv8[0:8, 0:1], sg8[:], ALU.mult, [kcv8, 'sg8'], [kcv8])
                osb, kosb = fpool.get()
                stt(osb[0:8, :], psO[0][0:8, :], cv8[0:8, 0:1], bmask[:], ALU.mult, ALU.mult, ['psO0', kcv8, 'bmask'], [kosb])
                mm(psO[1][0:4, :], oh84[:, r, :], osb[0:8, :], r == 0, r == 3, [kosb, 'oh84'], ['psO1'])
            a4, ka4 = fpool.get()
            cp('dve', a4[0:4, :], psO[1][0:4, :], ['psO1'], [ka4])
            for hp in range(4):
                subln_store(4, a4[0:4, hp * 128:hp * 128 + 128], ka4, hp, 0, [aTk(0)])
            if debug:
                P.dma('sp', lambda e: e.dma_start(out=dbgS.rearrange("k p t -> p k t"), in_=aT[:, :, 0:4]), reads=[aTk(0)])
            out_proj(w_out_ab[:, :], 1, 4, Rts)
            if debug:
                st_(dbgRs[0], Rs, [RsK])
            ffn_block(0, 1, 4, Rts, 'ffn0')
            if debug:
                st_(dbgRs[1], Rs, [RsK])
            load_gain('mix1')
            rmsnorm_T(4, Rs, RsK, 0, aTk(0))
            sg = sb([128, 3, 4, 4])
            for part in range(3):
                w, kw = load_w(w_in_cd[:, :], part * 512, 512)
                for blk in range(4):
                    ps, kps = lin_feat(4, aT, 0, [aTk(0)], w, kw, blk * 128)
                    cp('act', sg[:, part, blk, :], ps[:, 0:4], [kps], [('sg', part, blk)])
            csT = sb([128, 4, 12])
            lrT = sb([128, 4, 4])
            psT = sb([128, 4, 60])
            cso = sb([128, 4, 12])
            pso = sb([128, 4, 60])
            hso = sb([128, 4, 4])
            c2 = st_conv.rearrange("r j c -> (r j) c")
            p2 = st_pool.rearrange("r j c -> (r j) c")
            for blk in range(4):
                bs = slice(blk * 128, blk * 128 + 128)
                cols_from_rows(csT[:, blk, :], c2[:, bs], 12, [], ('csT', blk))
                cols_from_rows(lrT[:, blk, :], st_lru[:, bs], 4, [], ('lrT', blk))
                cols_from_rows(psT[:, blk, :], p2[:, bs], 60, [], ('psT', blk))
                cs3 = csT[:, blk, :].rearrange("p (r j) -> p r j", r=4)
                xl = sg[:, 1, blk, :]
                xc, kxc = fpool.get()
                ts('dve', xc[:, 0:4], cs3[:, :, 0], cwc[:, 0, blk:blk + 1], ALU.mult, cbc[:, blk:blk + 1], ALU.add, reads=[('csT', blk)] + cwT + ['cbc'], writes=[kxc])
                for j in (1, 2):
                    stt(xc[:, 0:4], cs3[:, :, j], cwc[:, j, blk:blk + 1], xc[:, 0:4], ALU.mult, ALU.add, [('csT', blk)] + cwT + [kxc], [kxc])
                stt(xc[:, 0:4], xl, cwc[:, 3, blk:blk + 1], xc[:, 0:4], ALU.mult, ALU.add, [('sg', 1, blk)] + cwT + [kxc], [kxc])
                co3 = cso[:, blk, :].rearrange("p (r j) -> p r j", r=4)
                cp('dve', co3[:, :, 0:2], cs3[:, :, 1:3], [('csT', blk)], [('cso', blk)])
                cp('dve', co3[:, :, 2], xl, [('sg', 1, blk), ('cso', blk)], [('cso', blk)])
                rows_out(conv_s.rearrange("r j c -> (r j) c")[:, bs], cso[:, blk, :], 12, [('cso', blk)])
                aa, kaa, bb, kbb = lru_gates(4, xc[:, 0:4], kxc, blk)
                tt('dve', hso[:, blk, :], aa[:, 0:4], lrT[:, blk, :], ALU.mult, [kaa, ('lrT', blk)], [('hso', blk)])
                tt('dve', hso[:, blk, :], hso[:, blk, :], bb[:, 0:4], ALU.add, [kbb, ('hso', blk)], [('hso', blk)])
                rows_out(lru_s[:, bs], hso[:, blk, :], 4, [('hso', blk)])
                gl_, kgl = fpool.get()
                gelu_tanh(gl_[:, 0:4], sg[:, 0, blk, :], 128, 4, [('sg', 0, blk)], [kgl])
                tt('dve', aT[:, blk, 0:4], gl_[:, 0:4], hso[:, blk, :], ALU.mult, [kgl, ('hso', blk)], [aTk(0)])
                wdw = 2 ** (blk + 1)
                ps3 = psT[:, blk, :].rearrange("p (r j) -> p r j", r=4)
                xpv = sg[:, 2, blk, :]
                wsum, kws = fpool.get()
                P.op('dve', lambda e, wsum=wsum, ps3=ps3, wdw=wdw: e.tensor_reduce(out=wsum[:, 0:4], in_=ps3[:, :, 16 - wdw:15], axis=AX.X, op=ALU.add), [('psT', blk)], [kws])
                tt('dve', wsum[:, 0:4], wsum[:, 0:4], xpv, ALU.add, [kws, ('sg', 2, blk)], [kws])
                plb, kplb = hpool.get()
                stt(plb[:, 0:4], wsum[:, 0:4], 1.0 / wdw, xpv, ALU.mult, ALU.subtract, [kws, ('sg', 2, blk)], [kplb])
                pp, kpp = pspool.get()
                mm(pp[:, 0:4], PWb[:, blk, :], plb[:, 0:4], True, True, ['PWb', kplb], [kpp])
                ts('dve', aT[:, 4 + blk, 0:4], pp[:, 0:4], pscc[:, blk:blk + 1], ALU.mult, reads=[kpp, 'pscc'], writes=[aTk(0)])
                po3 = pso[:, blk, :].rearrange("p (r j) -> p r j", r=4)
                cp('dve', po3[:, :, 0:14], ps3[:, :, 1:15], [('psT', blk)], [('pso', blk)])
                cp('dve', po3[:, :, 14], xpv, [('sg', 2, blk), ('pso', blk)], [('pso', blk)])
                rows_out(pool_s.rearrange("r j c -> (r j) c")[:, bs], pso[:, blk, :], 60, [('pso', blk)])
            out_proj(w_out_cd[:, :], 1, 4, Rts)
            ffn_block(1, 1, 4, Rts, 'ffn1')
            load_gain('fin')
            final_norm_out(4, Rs, RsK, y_s[:, :])

        P.emit()
    return nc, dt_in


_CACHE = {}


def _consts():
    c = {}
    c["c_ident"] = np.eye(128, dtype=np.float32)
    tri = np.zeros((128, 129), np.float32)
    for s in range(128):
        tri[s, s:128] = 1.0
    tri[:, 128] = 1.0
    c["c_trie"] = tri
    sel2 = np.zeros((128, 128), np.float32)
    for m in range(64):
        sel2[m + 64, m] = -1.0
        sel2[m, m + 64] = 1.0
    c["c_sel2"] = sel2
    gm = np.zeros((128, 8), np.float32)
    for p in range(128):
        gm[p, p // 16] = 1.0
    c["c_gmask"] = gm
    c["c_iota"] = np.tile(np.arange(640, dtype=np.float32)[None, :], (128, 1))
    c["c_part"] = np.arange(128, dtype=np.float32)[:, None].copy()
    bm = np.zeros((8, 512), np.float32)
    for h in range(8):
        bm[h, (h // 2) * 128:(h // 2) * 128 + 128] = 1.0
    c["c_bmask"] = bm
    oh84 = np.zeros((8, 4, 4), np.float32)
    oh44 = np.zeros((4, 4, 128), np.float32)
    oh14 = np.zeros((1, 4, 4), np.float32)
    for r in range(4):
        oh84[:, r, r] = 1.0
        oh44[r, r, :] = 1.0
        oh14[0, r, r] = 1.0
    c["c_oh84"], c["c_oh44"], c["c_oh14"] = oh84, oh44, oh14
    c["c_alt"] = (np.arange(8) % 2).astype(np.float32)[:, None].copy()
    return c


def _run(inputs, n_st=NST, do_sample=True, trace=False, debug=False, compact=False):
    key = (n_st, do_sample, debug, compact)
    if key not in _CACHE:
        _CACHE[key] = build(n_st, do_sample, debug, 512 if compact else NPOOL)
    nc, dt_in = _CACHE[key]
    f = lambda a: np.ascontiguousarray(np.asarray(a))
    I = {k: f(v) for k, v in inputs.items()}
    consts = _consts()
    shared = {
        "norm_mix": I["norm_mix"], "norm_ffn": I["norm_ffn"], "norm_final": I["norm_final"],
        "w_in_ab": I["w_in_ab"][0], "w_out_ab": I["w_out_ab"][0],
        "s5_a_re": I["s5_a_re"][0], "s5_a_im": I["s5_a_im"][0], "s5_log_dt": I["s5_log_dt"][0],
        "s5_b_re": I["s5_b_re"][0], "s5_b_im": I["s5_b_im"][0], "s5_c_re": I["s5_c_re"][0], "s5_c_im": I["s5_c_im"][0],
        "s5_d": I["s5_d"][0], "s5_w_glu": I["s5_w_glu"][0], "s5_b_glu": I["s5_b_glu"][0],
        "diff_lq1": I["diff_lq1"][0], "diff_lk1": I["diff_lk1"][0], "diff_lq2": I["diff_lq2"][0], "diff_lk2": I["diff_lk2"][0],
        "diff_subln": I["diff_subln"][0],
        "w_in_cd": I["w_in_cd"][0], "w_out_cd": I["w_out_cd"][0], "conv_w": I["conv_w"][0], "conv_b": I["conv_b"][0],
        "lru_wa": I["lru_wa"][0], "lru_ba": I["lru_ba"][0], "lru_wx": I["lru_wx"][0], "lru_bx": I["lru_bx"][0],
        "lru_lambda": I["lru_lambda"][0], "pool_w": I["pool_w"][0], "pool_scale": I["pool_scale"][0],
        "ffn_w_gate": I["ffn_w_gate"], "ffn_w_up": I["ffn_w_up"], "ffn_w_down": I["ffn_w_down"],
    }
    shared.update(consts)
    if do_sample and not compact:
        shared["ck"] = I["cache_k"][0].reshape(NPOOL, 128 * 512)
        shared["cv"] = I["cache_v"][0].reshape(NPOOL, 128 * 512)
    in_maps = []
    for c in range(8):
        m = dict(shared)
        m["xp"] = I["x_prompt"][c % 2]
        sl = slice(4 * c, 4 * c + 4)
        m["xs"] = I["x_sample"][sl, 0, :]
        m["pt"] = np.ascontiguousarray(I["page_table"][sl].T.astype(np.int32))
        if compact and do_sample:
            pg = I["page_table"][sl].reshape(-1)
            m["ck"] = I["cache_k"][0][pg].reshape(512, 128 * 512)
            m["cv"] = I["cache_v"][0][pg].reshape(512, 128 * 512)
            m["pt"] = np.ascontiguousarray(np.arange(512, dtype=np.int32).reshape(4, 128).T)
        m["st_re"] = I["state_s5_re"][0][sl]
        m["st_im"] = I["state_s5_im"][0][sl]
        m["st_conv"] = I["state_conv"][0][sl]
        m["st_lru"] = I["state_lru"][0][sl]
        m["st_pool"] = I["state_pool"][0][sl]
        m = {k: np.ascontiguousarray(v) for k, v in m.items() if k in dt_in}
        in_maps.append(m)
    res = run_bass_kernel_spmd(nc, in_maps, core_ids=list(range(8)), **({"trace": True} if trace else {}))
    R = res.results
    L = n_st * 512

    def pr(name, shape):
        return np.stack([R[b][name].reshape(shape) for b in range(2)])[None] if True else None
    y_prompt = np.stack([R[b]["y_p"] for b in range(2)])
    y_sample = np.concatenate([R[c]["y_s"] for c in range(8)], 0)[:, None, :]
    k_prompt = np.stack([R[b]["k_p"].reshape(SEQ, 8, 64) for b in range(2)])[None]
    v_prompt = np.stack([R[b]["v_p"].reshape(SEQ, 4, 128) for b in range(2)])[None]
    k_sample = np.concatenate([R[c]["k_s"] for c in range(8)], 0).reshape(1, 32, 1, 8, 64)
    v_sample = np.concatenate([R[c]["v_s"] for c in range(8)], 0).reshape(1, 32, 1, 4, 128)
    re_p = np.stack([R[b]["re_p"] for b in range(2)])[None]
    im_p = np.stack([R[b]["im_p"] for b in range(2)])[None]
    re_s = np.concatenate([R[c]["re_s"] for c in range(8)], 0)[None]
    im_s = np.concatenate([R[c]["im_s"] for c in range(8)], 0)[None]
    conv_p = np.stack([R[b]["conv_p"] for b in range(2)])[None]
    conv_s = np.concatenate([R[c]["conv_s"] for c in range(8)], 0)[None]
    lru_p = np.stack([R[b]["lru_p"] for b in range(2)])[None]
    lru_s = np.concatenate([R[c]["lru_s"] for c in range(8)], 0)[None]
    pool_p = np.stack([R[b]["pool_p"] for b in range(2)])[None]
    pool_s = np.concatenate([R[c]["pool_s"] for c in range(8)], 0)[None]
    outs = (y_prompt, y_sample, k_prompt, v_prompt, k_sample, v_sample, re_p, im_p, re_s, im_s,
            conv_p, conv_s, lru_p, lru_s, pool_p, pool_s)
    outs = tuple(np.ascontiguousarray(o.astype(np.float32)) for o in outs)
    return outs, res


def kernel(**inputs):
    outs, _ = _run(inputs)
    return outs
```
